# Optimizing a Trainium2 kernel written in Bass

```python
import jax, jax.numpy as jnp
from jax import lax
import numpy as np

D_MODEL = 1024
BATCH = 8
SEQ = 4096
DEPTH = 2
DEC_BATCH = 1
DEC_SEQ = 16384
PAST_LEN = 128

GRID_W = 64
N_HEADS = 16
HEAD_DIM = D_MODEL // N_HEADS
MAX_KH = 8
KW = 16
KB = min(2 * KW, GRID_W)
CONV_W = 3
D_FF = 2816
N_MIXERS = 2
N_NA = (DEPTH + 1) // 2
N_SC = DEPTH // 2
EPS = 1e-6
NEG_INF = -1e30

kernel_name = "hybrid_natten_shortconv_encoder"


def rmsnorm(x, g):
    x32 = x.astype(jnp.float32)
    y = x32 * lax.rsqrt(jnp.mean(x32 * x32, axis=-1, keepdims=True) + EPS)
    return (y * g.astype(jnp.float32)).astype(x.dtype)


def conv3_centered(x, w):
    xp = jnp.pad(x, ((0, 0), (1, 1), (0, 0)))
    return xp[:, :-2] * w[0] + xp[:, 1:-1] * w[1] + xp[:, 2:] * w[2]


def neighbourhood_attention(q, k, v, rpb):
    bsz, t, h, dh = q.shape
    rows = t // GRID_W
    kh = min(MAX_KH, rows)
    ncb = GRID_W // KW
    scale = dh ** -0.5
    qg = q.reshape(bsz, rows, GRID_W, h, dh).transpose(0, 3, 1, 2, 4)
    kg = k.reshape(bsz, rows, GRID_W, h, dh).transpose(0, 3, 1, 2, 4)
    vg = v.reshape(bsz, rows, GRID_W, h, dh).transpose(0, 3, 1, 2, 4)
    cols = np.arange(GRID_W)
    col_start = np.clip(cols - KW // 2, 0, GRID_W - KW).reshape(ncb, KW)
    q_cols = cols.reshape(ncb, KW)
    blk_start = np.clip(np.arange(ncb) * KW - KW // 2, 0, GRID_W - KB)
    key_cols = blk_start[:, None] + np.arange(KB)[None, :]
    kc = key_cols[:, None, :]
    col_valid = (kc >= col_start[..., None]) & (kc < col_start[..., None] + KW)
    dc_idx = np.clip(kc - q_cols[..., None] + KW - 1, 0, 2 * KW - 2)
    mask = jnp.asarray(col_valid)[None, None, :, :, None, :]

    def row_block(r):
        rs = jnp.clip(r - kh // 2, 0, rows - kh)
        qr = lax.dynamic_index_in_dim(qg, r, axis=2, keepdims=False)
        kband = lax.dynamic_slice_in_dim(kg, rs, kh, axis=2)
        vband = lax.dynamic_slice_in_dim(vg, rs, kh, axis=2)
        kblk = kband[:, :, :, key_cols]
        vblk = vband[:, :, :, key_cols]
        qb = qr.reshape(bsz, h, ncb, KW, dh)
        s = jnp.einsum('bhjqd,bhijkd->bhjqik', qb, kblk,
                       preferred_element_type=jnp.float32) * scale
        dr = rs + jnp.arange(kh) - r
        rpb_r = rpb[:, dr + MAX_KH - 1]
        bias = rpb_r[:, :, dc_idx].transpose(0, 2, 3, 1, 4)
        s = jnp.where(mask, s + bias.astype(jnp.float32)[None], NEG_INF)
        p = jax.nn.softmax(s, axis=(-2, -1))
        o = jnp.einsum('bhjqik,bhijkd->bhjqd', p.astype(vblk.dtype), vblk)
        return o.reshape(bsz, h, GRID_W, dh)

    out = lax.map(row_block, jnp.arange(rows))
    return out.transpose(1, 0, 3, 2, 4).reshape(bsz, t, h * dh)


def na_mixer(x, w_qkv, b_qkv, rpb, w_o):
    bsz, t, _ = x.shape
    qkv = jnp.dot(x, w_qkv) + b_qkv
    q, k, v = jnp.split(qkv, 3, axis=-1)
    shp = (bsz, t, N_HEADS, HEAD_DIM)
    o = neighbourhood_attention(q.reshape(shp), k.reshape(shp), v.reshape(shp), rpb)
    return jnp.dot(o, w_o)


def short_conv_mixer(x, w_in, conv_w, w_out):
    bg, cg, u = jnp.split(jnp.dot(x, w_in), 3, axis=-1)
    y = conv3_centered(cg * u, conv_w)
    return jnp.dot(bg * y, w_out)


def conv_ffn(x, w_up, conv_w, conv_b, w_down):
    a, val = jnp.split(jnp.dot(x, w_up), 2, axis=-1)
    a = conv3_centered(a, conv_w) + conv_b
    return jnp.dot(jax.nn.silu(a) * val, w_down)


def trunk(x, c, p):
    for i in range(DEPTH):
        mod = jnp.dot(jax.nn.silu(c), p['ada_w'][i]) + p['ada_b'][i]
        sh1, sc1, g1, sh2, sc2, g2 = jnp.split(mod[:, None, :], 6, axis=-1)
        h = rmsnorm(x, p['norm1_g'][i]) * (1 + sc1) + sh1
        j = i // N_MIXERS
        if i % N_MIXERS == 0:
            m = na_mixer(h, p['na_w_qkv'][j], p['na_b_qkv'][j], p['na_rpb'][j], p['na_w_o'][j])
        else:
            m = short_conv_mixer(h, p['sc_w_in'][j], p['sc_conv_w'][j], p['sc_w_out'][j])
        x = x + g1 * m
        h = rmsnorm(x, p['norm2_g'][i]) * (1 + sc2) + sh2
        x = x + g2 * conv_ffn(h, p['ffn_w_up'][i], p['ffn_conv_w'][i], p['ffn_conv_b'][i], p['ffn_w_down'][i])
    return rmsnorm(x, p['final_g'])


def setup_inputs(seed: int = 0) -> dict:
    key = jax.random.key(seed)
    ks = jax.random.split(key, 20)
    D, F = D_MODEL, D_FF
    nrm = jax.random.normal
    return {
        'x_prompt': nrm(ks[0], (BATCH, SEQ, D), jnp.float32),
        'x_sample': nrm(ks[1], (DEC_BATCH, DEC_SEQ, D), jnp.float32),
        'c_prompt': nrm(ks[2], (BATCH, D), jnp.float32),
        'c_sample': nrm(ks[3], (DEC_BATCH, D), jnp.float32),
        'ada_w': nrm(ks[4], (DEPTH, D, 6 * D), jnp.float32) * (0.5 * D ** -0.5),
        'ada_b': nrm(ks[5], (DEPTH, 6 * D), jnp.float32) * 0.02,
        'norm1_g': 1.0 + 0.02 * nrm(ks[6], (DEPTH, D), jnp.float32),
        'norm2_g': 1.0 + 0.02 * nrm(ks[7], (DEPTH, D), jnp.float32),
        'na_w_qkv': nrm(ks[8], (N_NA, D, 3 * D), jnp.float32) * D ** -0.5,
        'na_b_qkv': nrm(ks[9], (N_NA, 3 * D), jnp.float32) * 0.02,
        'na_rpb': nrm(ks[10], (N_NA, N_HEADS, 2 * MAX_KH - 1, 2 * KW - 1), jnp.float32) * 0.1,
        'na_w_o': nrm(ks[11], (N_NA, D, D), jnp.float32) * D ** -0.5,
        'sc_w_in': nrm(ks[12], (N_SC, D, 3 * D), jnp.float32) * D ** -0.5,
        'sc_conv_w': nrm(ks[13], (N_SC, CONV_W, D), jnp.float32) * CONV_W ** -0.5,
        'sc_w_out': nrm(ks[14], (N_SC, D, D), jnp.float32) * D ** -0.5,
        'ffn_w_up': nrm(ks[15], (DEPTH, D, 2 * F), jnp.float32) * D ** -0.5,
        'ffn_conv_w': nrm(ks[16], (DEPTH, CONV_W, F), jnp.float32) * CONV_W ** -0.5,
        'ffn_conv_b': nrm(ks[17], (DEPTH, F), jnp.float32) * 0.02,
        'ffn_w_down': nrm(ks[18], (DEPTH, F, D), jnp.float32) * F ** -0.5,
        'final_g': 1.0 + 0.02 * nrm(ks[19], (D,), jnp.float32),
    }


def reference(x_prompt, x_sample, c_prompt, c_sample, ada_w, ada_b, norm1_g, norm2_g,
              na_w_qkv, na_b_qkv, na_rpb, na_w_o, sc_w_in, sc_conv_w, sc_w_out,
              ffn_w_up, ffn_conv_w, ffn_conv_b, ffn_w_down, final_g):
    p = dict(ada_w=ada_w, ada_b=ada_b, norm1_g=norm1_g, norm2_g=norm2_g,
             na_w_qkv=na_w_qkv, na_b_qkv=na_b_qkv, na_rpb=na_rpb, na_w_o=na_w_o,
             sc_w_in=sc_w_in, sc_conv_w=sc_conv_w, sc_w_out=sc_w_out,
             ffn_w_up=ffn_w_up, ffn_conv_w=ffn_conv_w, ffn_conv_b=ffn_conv_b,
             ffn_w_down=ffn_w_down, final_g=final_g)
    y_prompt = trunk(x_prompt, c_prompt, p)
    y_sample = trunk(x_sample, c_sample, p)
    return (y_prompt, y_sample)
```

```python
import numpy as np
import ml_dtypes
from contextlib import ExitStack
import concourse.bass as bass
import concourse.mybir as mybir
from concourse.bass_utils import run_bass_kernel_spmd

F32 = mybir.dt.float32
BF16 = mybir.dt.bfloat16
AF = mybir.ActivationFunctionType
ALU = mybir.AluOpType

D = 1024
NH = 16
FF = 2816
NJ = FF // 128
GW = 64
EPS = 1e-6
NEG = -30000.0

TR = 16
TT = TR * GW
HR = 5
EXT = (TR + 2 * HR) * GW
NCH = EXT // 128
QOFF = 256
NQ = 1152
NPAIR = 9
RES0 = 317
NRES = TT + 6
P_ROWS = 64
S_ROWS = 256
S_LOC = 32
NSLOT = 18
NB_EXT = 208
RES_BLK = [(0, 344), (344, 688), (688, 1030)]
NRM_BLK = [(0, 206), (206, 412), (412, 618), (618, 824), (824, 1030)]

W_Q, W_K, W_V, W_O, W_IN, W_OUT, W_UP0, W_UP1, NW8 = 0, 8, 16, 24, 32, 56, 64, 108, 152

V_ADAB = 0
V_N1 = 96
V_N2 = 112
V_FG = 128
V_BQK = 136
V_SCW = 152
V_FCW = 176
V_FCB = 308
NV = 352

COMPUTE_Q = ("pe", "act", "dve", "pool")
N_DMA_SEMS = 24


class Prog:
    def __init__(self, nc):
        self.nc = nc
        self.q = {k: [] for k in ("pe", "act", "dve", "pool", "sp")}
        self.res = {}
        self.dma_cnt = [0] * N_DMA_SEMS
        self.dma_rr = 0

    def _collect(self, reads, writes, me_q, is_dma):
        waits = set()
        for r in reads:
            st = self.res.get(r)
            if st is not None and st["w"] is not None:
                waits.add(st["w"])
        for wname in writes:
            st = self.res.get(wname)
            if st is None:
                continue
            w = st["w"]
            if w is not None and (is_dma or not (w[0] == "c" and w[1] == me_q)):
                waits.add(w)
            for qn, idx in st["rc"].items():
                if is_dma or qn != me_q:
                    waits.add(("c", qn, idx))
            for d in st["rd"]:
                waits.add(d)
        if not is_dma and me_q == "pe":
            waits = {w for w in waits if not (w[0] == "c" and w[1] == "pe")}
        return waits

    def _record(self, reads, writes, me):
        for r in reads:
            st = self.res.setdefault(r, {"w": None, "rc": {}, "rd": []})
            if me[0] == "c":
                st["rc"][me[1]] = me[2]
            else:
                st["rd"].append(me)
        for wname in writes:
            self.res[wname] = {"w": me, "rc": {}, "rd": []}

    def op(self, q, fn, reads=(), writes=()):
        waits = self._collect(reads, writes, q, False)
        idx = len(self.q[q])
        self.q[q].append({"fn": fn, "waits": waits, "inc": False, "dma": None})
        self._record(reads, writes, ("c", q, idx))
        return idx

    def dma(self, q, fn, reads=(), writes=()):
        waits = self._collect(reads, writes, q, True)
        k = self.dma_rr
        self.dma_rr = (self.dma_rr + 1) % N_DMA_SEMS
        if self.dma_cnt[k] > 0:
            waits.add(("d", k, 16 * self.dma_cnt[k]))
        self.dma_cnt[k] += 1
        me = ("d", k, 16 * self.dma_cnt[k])
        self.q[q].append({"fn": fn, "waits": waits, "inc": False, "dma": k})
        self._record(reads, writes, me)
        return me

    def emit(self):
        nc = self.nc
        for qn, ops in self.q.items():
            for o in ops:
                for w in o["waits"]:
                    if w[0] == "c":
                        self.q[w[1]][w[2]]["inc"] = True
        val = {}
        for qn, ops in self.q.items():
            c = 0
            for i, o in enumerate(ops):
                if o["dma"] is None and o["inc"]:
                    c += 1
                    val[(qn, i)] = c
        with ExitStack() as es:
            csem = {qn: es.enter_context(nc.semaphore("s_" + qn)) for qn in COMPUTE_Q}
            dsem = [es.enter_context(nc.semaphore("d%d" % i)) for i in range(N_DMA_SEMS)]
            block = es.enter_context(nc.Block())

            def run(qn, eng):
                seen_c = {}
                seen_d = {}
                for o in self.q[qn]:
                    for w in sorted(o["waits"]):
                        if w[0] == "c":
                            v = val[(w[1], w[2])]
                            if seen_c.get(w[1], 0) >= v:
                                continue
                            seen_c[w[1]] = v
                            eng.wait_ge(csem[w[1]], v)
                        else:
                            if seen_d.get(w[1], 0) >= w[2]:
                                continue
                            seen_d[w[1]] = w[2]
                            eng.wait_ge(dsem[w[1]], w[2])
                    ins = o["fn"](eng)
                    if o["dma"] is not None:
                        ins.then_inc(dsem[o["dma"]], 16)
                    elif o["inc"]:
                        ins.then_inc(csem[qn], 1)
                if qn == "sp":
                    for k in range(N_DMA_SEMS):
                        if self.dma_cnt[k] > 0:
                            eng.wait_ge(dsem[k], 16 * self.dma_cnt[k])

            @block.tensor
            def _(e):
                run("pe", e)

            @block.scalar
            def _(e):
                run("act", e)

            @block.vector
            def _(e):
                run("dve", e)

            @block.gpsimd
            def _(e):
                run("pool", e)

            @block.sync
            def _(e):
                run("sp", e)


def _win(r, rows):
    r = min(max(r, 0), rows - 1)
    rs = min(max(r - 4, 0), rows - 8)
    return rs, rs + 8


def _pair_valid(R0, rows, j):
    rA = R0 - 1 + 2 * j
    out = {}
    for c in range(NCH):
        k0 = R0 - 5 + 2 * c
        v = [[False, False], [False, False]]
        for qi, r in enumerate((rA, rA + 1)):
            lo, hi = _win(r, rows)
            for ki, k in enumerate((k0, k0 + 1)):
                v[qi][ki] = lo <= k < hi
        out[c] = v
    return out


def _pair_plan_static(R0, rows, j):
    val = _pair_valid(R0, rows, j)
    used = [c for c in range(NCH) if any(val[c][0]) or any(val[c][1])]
    c_lo, c_hi = min(used), max(used)
    n = c_hi - c_lo + 1
    s0 = 12 - 2 * (c_hi - j)
    assert 0 <= s0 and s0 + 2 * n <= NSLOT, (R0, rows, j, s0, n)
    rows_pv = [[], []]
    for i in range(n):
        c = c_hi - i
        for qi in range(2):
            t, b = val[c][qi]
            if t and b:
                rows_pv[qi].append((i, "both"))
            elif t:
                rows_pv[qi].append((i, "top"))
            elif b:
                rows_pv[qi].append((i, "bot"))
    return dict(c_hi=c_hi, n=n, s0=s0, pv=rows_pv, dyn=None)


def _sample_plans():
    plans = []
    masks = []
    for t in range(2):
        tp = []
        for j in range(NPAIR):
            per_core = [_pair_valid(S_LOC * c + TR * t, S_ROWS, j) for c in range(8)]
            same = all(per_core[c] == per_core[1] for c in range(8))
            if same:
                tp.append(_pair_plan_static(S_LOC * 1 + TR * t, S_ROWS, j))
                continue
            used = sorted({cc for pc in per_core for cc in range(NCH)
                           if any(pc[cc][0]) or any(pc[cc][1])})
            c_lo, c_hi = used[0], used[-1]
            n = c_hi - c_lo + 1
            assert n <= 7
            s0 = 12 - 2 * (c_hi - j)
            assert 0 <= s0 and s0 + 2 * n <= NSLOT
            m = np.zeros((8, 128, 896), np.float32)
            for c in range(8):
                for i in range(n):
                    cc = c_hi - i
                    for qi in range(2):
                        for ki in range(2):
                            if per_core[c][cc][qi][ki]:
                                m[c, ki * 64:(ki + 1) * 64, i * 128 + qi * 64:i * 128 + qi * 64 + 64] = 1.0
            tp.append(dict(c_hi=c_hi, n=n, s0=s0, pv=None, dyn=len(masks)))
            masks.append(m)
        plans.append(tp)
    return plans, masks


def build_program(tiles=None, debug=False):
    if tiles is None:
        tiles = [(0, t) for t in range(4)] + [(1, t) for t in range(2)]
    nc = bass.Bass("TRN2", target_bir_lowering=False)
    dr = {}
    splans, _masks = _sample_plans()
    n_dyn = len(_masks)

    def din(name, shape, dt=F32):
        dr[name] = nc.dram_tensor(name, list(shape), dt, kind="ExternalInput").ap()
        return dr[name]

    xp = din("xp", [128, 8, P_ROWS * GW + 2 * HR * GW])
    xs = din("xs", [128, 8, S_LOC * GW + 2 * HR * GW])
    cv = din("cv", [128, 8, 2])
    pcm = din("pcm", [128, 4])
    mt_in = din("mt", [128, n_dyn, 896])
    vecs_in = din("vecs", [128, NV])
    bvb_in = din("bvb", [128, D])
    ident_in = din("ident", [128, 128])
    btab_in = din("btab", [128, NH, NSLOT * 64])
    ada_in = din("ada", [96, 128, 1024])
    w8_in = din("w8", [NW8, 128, 1024])
    wd_in = din("wd", [16, 128, NJ * 128])
    yp = nc.dram_tensor("yp", [128, 8, P_ROWS * GW], F32, kind="ExternalOutput").ap()
    ys = nc.dram_tensor("ys", [128, 8, S_LOC * GW], F32, kind="ExternalOutput").ap()
    w8b = nc.dram_tensor("w8b", [NW8, 128, 1024], BF16, kind="Internal").ap()
    wdb = nc.dram_tensor("wdb", [16, 128, NJ * 128], BF16, kind="Internal").ap()
    tabb = nc.dram_tensor("tabb", [128, NH, NSLOT * 64], BF16, kind="Internal").ap()
    mtb = nc.dram_tensor("mtb", [128, n_dyn, 896], BF16, kind="Internal").ap()

    P = Prog(nc)

    with ExitStack() as es:
        def sb(name, shape, dt):
            return es.enter_context(nc.sbuf_tensor("sb_" + name, list(shape), dt))

        x_res = sb("x_res", [128, 8, NRES], F32)
        h_ext = sb("h_ext", [128, 8, EXT + 2], BF16)
        xblk = [sb("xblk%d" % i, [128, 8, 344], F32) for i in range(2)]
        rstd = sb("rstd", [128, EXT], F32)
        epsb = sb("epsb", [128, 1], F32)
        w8s = [sb("w8s%d" % i, [128, 8, 128], BF16) for i in range(5)]
        vecs = sb("vecs", [128, NV], F32)
        ident = sb("ident", [128, 128], BF16)
        ones = sb("ones", [128, 128], BF16)
        pcs = sb("pcs", [128, 4], F32)
        MOD = sb("MOD", [128, 2, 48, 2], F32)
        GM = sb("GM", [128, 2, 2, 2, 8], F32)
        G32 = sb("G32", [128, 3, 8], F32)
        FG32 = sb("FG32", [128, 8], F32)
        csil = sb("csil", [128, 8, 2], F32)
        fscr = sb("fscr", [128, 4], F32)
        rec = [sb("rec%d" % i, [128, 2], F32) for i in range(2)]
        bcolb = [sb("bcol%d" % i, [128, 1], F32) for i in range(4)]
        RX_ELEMS = 49000
        RX = sb("RX", [128, RX_ELEMS], BF16)
        ps = es.enter_context(nc.psum_tensor("ps", [128, 7 * 512], F32))
        psT = es.enter_context(nc.psum_tensor("psT", [128, 1024], BF16))

        off = [0]

        def carve(n_elems_bf16):
            o = off[0]
            off[0] += n_elems_bf16
            assert off[0] <= RX_ELEMS, off[0]
            return o

        def v_bf(o, n):
            return RX[:, o:o + n]

        def v_f32(o, n):
            return RX[:, o:o + 2 * n].bitcast(F32)

        offA_QT = [carve(NQ) for _ in range(2)]
        offA_KT = [carve(EXT) for _ in range(2)]
        offA_V = [carve(NCH * 130) for _ in range(2)]
        offA_OSB = carve(NPAIR * D)
        offA_TAB = [carve(2 * NSLOT * 64) for _ in range(2)]
        offA_E = [carve(896) for _ in range(2)]
        offA_P = [carve(896) for _ in range(2)]
        offA_OT = carve(8 * NQ)
        offA_MTS = carve(n_dyn * 896)
        offA_BVB = carve(2 * D)
        offA_STG = carve(2 * 2 * NSLOT * 64)
        endA = off[0]
        off[0] = 0
        offB_G = carve(NJ * NRES)
        offB_Z = carve(8 * NRES)
        offB_T = [[carve(2 * NRES + 8) for _ in range(3)] for _ in range(2)]
        offB_WD = [carve(NJ * 128) for _ in range(2)]
        endB = off[0]
        off[0] = 0

        QT = [v_bf(o, NQ) for o in offA_QT]
        KT = [v_bf(o, EXT) for o in offA_KT]
        VA = [v_bf(o, NCH * 130).rearrange("p (m h d) -> p m h d", h=2, d=65) for o in offA_V]
        OSB = v_bf(offA_OSB, NPAIR * D).rearrange("p (j f) -> p j f", f=D)
        TAB = [v_bf(o, 2 * NSLOT * 64).rearrange("p (h s) -> p h s", h=2) for o in offA_TAB]
        EB = [v_bf(o, 896) for o in offA_E]
        PB = [v_bf(o, 896) for o in offA_P]
        OT = v_bf(offA_OT, 8 * NQ).rearrange("p (k q) -> p k q", k=8)
        MTS = v_bf(offA_MTS, n_dyn * 896).rearrange("p (a b) -> p a b", a=n_dyn)
        BVB = v_f32(offA_BVB, D)
        STG = v_f32(offA_STG, 2 * NSLOT * 64).rearrange("p (h s) -> p h s", h=2)
        GB = v_bf(offB_G, NJ * NRES).rearrange("p (j n) -> p j n", j=NJ)
        ZB = v_bf(offB_Z, 8 * NRES).rearrange("p (k n) -> p k n", k=8)
        TMP = [[v_f32(o, NRES + 4) for o in oo] for oo in offB_T]
        WDS = [v_bf(o, NJ * 128).rearrange("p (j c) -> p j c", j=NJ) for o in offB_WD]

        RXL = "RX"
        dbg_outs = {}

        def dump(name, ap, reads):
            if not debug:
                return
            t = nc.dram_tensor("dbg_" + name, list(ap.shape), ap.dtype, kind="ExternalOutput").ap()
            dbg_outs[name] = t
            P.dma("sp", lambda e: e.dma_start(out=t, in_=ap), reads=list(reads) + [RXL], writes=[("dbg", name)])

        def fence():
            P.op("dve", lambda e: e.memset(fscr[:, 0:1], 0.0), reads=[], writes=[RXL])

        def vcol(c):
            return vecs[:, c:c + 1]

        bank_rr = [0]

        def bank(banks=(0, 1, 2, 3, 4, 5, 6)):
            b = banks[bank_rr[0] % len(banks)]
            bank_rr[0] += 1
            return b

        def pbank(b, n=512, p0=0, p1=128):
            return ps[p0:p1, b * 512:b * 512 + n]

        stream = []
        for (seg, t) in tiles:
            for c in range(8):
                stream += [("w8", W_Q + c), ("w8", W_K + c), ("w8", W_V + c)]
            stream += [("w8", W_O + o) for o in range(8)]
            for j in range(NJ):
                stream += [("w8", W_UP0 + j), ("w8", W_UP0 + NJ + j)]
            stream += [("wd", o) for o in range(8)]
            for k in range(8):
                stream += [("w8", W_IN + 8 + k), ("w8", W_IN + 16 + k), ("w8", W_IN + k)]
            stream += [("w8", W_OUT + o) for o in range(8)]
            for j in range(NJ):
                stream += [("w8", W_UP1 + j), ("w8", W_UP1 + NJ + j)]
            stream += [("wd", 8 + o) for o in range(8)]
        st = {"pos": 0, "emitted": 0, "n_w8": 0, "n_wd": 0, "slot": {}}
        LOOK = 4

        def _emit_load(i):
            kind, idx = stream[i]
            ensure_casts((kind, idx // 8 if kind == "w8" else idx // 4), 2)
            if kind == "w8":
                s = st["n_w8"] % 5
                st["n_w8"] += 1
                P.dma("sp", lambda e, s=s, idx=idx: e.dma_start(
                    out=w8s[s][:, :, :], in_=w8b[idx].rearrange("p (k c) -> p k c", k=8)),
                    reads=[("w8b", idx // 8)], writes=[("w8s", s)])
            else:
                s = st["n_wd"] % 2
                st["n_wd"] += 1
                P.dma("sp", lambda e, s=s, idx=idx: e.dma_start(
                    out=WDS[s], in_=wdb[idx].rearrange("p (j c) -> p j c", j=NJ)),
                    reads=[("wdb", idx // 4), RXL], writes=[("wds", s)])
            st["slot"][i] = s

        def wnext(kind, idx):
            i = st["pos"]
            assert stream[i] == (kind, idx), (i, stream[i], kind, idx)
            lim = min(len(stream), i + LOOK + 1)
            used = st.setdefault("used", {"w8": 0, "wd": 0})
            pool_sz = {"w8": 5, "wd": 2}
            while st["emitted"] < lim:
                k2 = stream[st["emitted"]][0]
                if k2 == "wd" and not st.get("wd_ok", False) and st["emitted"] > i:
                    break
                if st["n_" + k2] - used[k2] >= pool_sz[k2]:
                    break
                _emit_load(st["emitted"])
                st["emitted"] += 1
            if st["emitted"] <= i:
                _emit_load(i)
                st["emitted"] = i + 1
            st["pos"] += 1
            used[kind] += 1
            s = st["slot"][i]
            if kind == "w8":
                return w8s[s], ("w8s", s)
            return WDS[s], ("wds", s)

        P.dma("sp", lambda e: e.dma_start(out=vecs[:, :], in_=vecs_in[:, :]), writes=["vecs"])
        P.dma("sp", lambda e: e.dma_start(out=pcs[:, :], in_=pcm[:, :]), writes=["pcs"])
        P.dma("sp", lambda e: e.dma_start(out=csil[:, :, :], in_=cv[:, :, :]), writes=["csil"])
        P.op("dve", lambda e: e.memset(ones[:, :], 1.0), writes=["ones"])
        P.op("dve", lambda e: e.memset(epsb[:, :], float(D * EPS)), writes=["epsb"])
        csb = sb("csb", [128, 8, 2], BF16)
        ada_st = [sb("adast%d" % i, [128, 8, 128], BF16) for i in range(3)]
        P.op("act", lambda e: e.activation(out=csb[:, :, :], in_=csil[:, :, :], func=AF.Silu),
             reads=["csil"], writes=["csb"])
        ada_cnt = [0]

        def ada_chunk(l, oc, fixed_bank=None):
            ch = l * 48 + oc
            b = ada_cnt[0] % 3
            ada_cnt[0] += 1
            P.dma("pool", lambda e: e.dma_start(out=ada_st[b][:, :, :], in_=ada_in[ch].rearrange("p (k c) -> p k c", k=8)),
                  writes=[("adast", b)])
            pb = fixed_bank if fixed_bank is not None else bank()
            for kc in range(8):
                P.op("pe", lambda e, kc=kc: e.matmul(
                    pbank(pb, 2), lhsT=ada_st[b][:, kc, :], rhs=csb[:, kc, :],
                    start=(kc == 0), stop=(kc == 7)),
                    reads=[("adast", b), "csb"], writes=[("ps", pb)])
            P.op("dve", lambda e: e.tensor_scalar(
                out=MOD[:, l, oc, :], in0=pbank(pb, 2), scalar1=vcol(V_ADAB + l * 48 + oc), scalar2=None,
                op0=ALU.add), reads=[("ps", pb), "vecs"], writes=["MOD"])

        P.op("dve", lambda e: e.tensor_scalar(out=G32[:, 0:2, :], in0=vecs[:, V_N1:V_N1 + 16].rearrange("p (l k) -> p l k", l=2),
                                              scalar1=32.0, scalar2=None, op0=ALU.mult), reads=["vecs"], writes=["G32a"])
        P.op("dve", lambda e: e.tensor_scalar(out=FG32[:, :], in0=vecs[:, V_FG:V_FG + 8],
                                              scalar1=32.0, scalar2=None, op0=ALU.mult), reads=["vecs"], writes=["FG32"])
        G32b = sb("G32b", [128, 2, 8], F32)
        P.op("dve", lambda e: e.tensor_scalar(out=G32b[:, :, :], in0=vecs[:, V_N2:V_N2 + 16].rearrange("p (l k) -> p l k", l=2),
                                              scalar1=32.0, scalar2=None, op0=ALU.mult), reads=["vecs"], writes=["G32b"])

        def gm_op(l, which):
            for s in range(2):
                if which == 0:
                    P.op("dve", lambda e, s=s: e.scalar_tensor_tensor(
                        out=GM[:, l, 0, s, :], in0=MOD[:, l, 8:16, s], scalar=1.0, in1=G32[:, l, :],
                        op0=ALU.add, op1=ALU.mult), reads=["MOD", "G32a"], writes=[("GM", l, 0, s)])
                else:
                    P.op("dve", lambda e, s=s: e.scalar_tensor_tensor(
                        out=GM[:, l, 1, s, :], in0=MOD[:, l, 32:40, s], scalar=1.0, in1=G32b[:, l, :],
                        op0=ALU.add, op1=ALU.mult), reads=["MOD", "G32b"], writes=[("GM", l, 1, s)])

        ada_rest = []
        for l in range(2):
            for oc in range(48):
                if l == 0 and oc < 16:
                    continue
                ada_rest.append(lambda fb, l=l, oc=oc: ada_chunk(l, oc, fb))
                if oc == 15:
                    ada_rest.append(lambda fb, l=l: gm_op(l, 0))
                if oc == 39:
                    ada_rest.append(lambda fb, l=l: gm_op(l, 1))
        for oc in range(16):
            ada_chunk(0, oc)
        gm_op(0, 0)
        cast_order = ([("w8", g) for g in (0, 1, 2, 3)] + [("w8", g) for g in range(8, 14)] + [("wd", 0), ("wd", 1)]
                      + [("w8", g) for g in (4, 5, 6, 7)] + [("w8", g) for g in range(14, 19)] + [("wd", 2), ("wd", 3)])
        cast_pos = {k: i for i, k in enumerate(cast_order)}
        cast_state = {"n": 0}

        def ensure_casts(key, ahead=2):
            lim = min(len(cast_order), cast_pos[key] + 1 + ahead)
            while cast_state["n"] < lim:
                kind, g = cast_order[cast_state["n"]]
                cast_state["n"] += 1
                if kind == "w8":
                    P.dma("pool", lambda e, g=g: e.dma_start(
                        out=w8b[g * 8:(g + 1) * 8].rearrange("n p c -> (n p) c"),
                        in_=w8_in[g * 8:(g + 1) * 8].rearrange("n p c -> (n p) c")),
                        reads=[], writes=[("w8b", g)])
                else:
                    P.dma("pool", lambda e, g=g: e.dma_start(
                        out=wdb[g * 4:(g + 1) * 4].rearrange("n p c -> (n p) c"),
                        in_=wd_in[g * 4:(g + 1) * 4].rearrange("n p c -> (n p) c")),
                        reads=[], writes=[("wdb", g)])

        ensure_casts(("w8", 0), 2)
        P.dma("pool", lambda e: e.dma_start(out=ident[:, :], in_=ident_in[:, :]), writes=["ident"])
        P.dma("pool", lambda e: e.dma_start(out=mtb[:, :, :], in_=mt_in[:, :, :]), writes=["mtb"])
        tab_done = [False] * 8
        MODR = ["MOD"] + [("GM", l, w, s) for l in range(2) for w in range(2) for s in range(2)]
        dump("MOD", MOD[:, :, :, :], MODR)
        dump("GM", GM[:, :, :, :, :].rearrange("p a b c d -> p (a b c d)"), MODR)

        def modcol(l, oc, s):
            return MOD[:, l, oc, s:s + 1]

        def hgran(c0, c1):
            return [("h", i) for i in range(c0 // 128, (c1 + 127) // 128)]

        nb_rr = [0]
        EXT_BLK = [(0, 344), (344, 688), (688, 1032), (1032, 1376), (1376, 1664)]

        def _affine(i, n, gm_fn, sh_fn, dst_fn, dst_res):
            for k in range(8):
                gm = gm_fn(k)
                sh = sh_fn(k)
                if sh is None:
                    P.op("act", lambda e, k=k, gm=gm: e.activation(out=dst_fn(k), in_=xblk[i][:, k, 0:n], func=AF.Identity, scale=gm),
                         reads=[("xb", i)] + MODR + ["FG32"], writes=dst_res)
                else:
                    P.op("act", lambda e, k=k, gm=gm, sh=sh: e.activation(out=dst_fn(k), in_=xblk[i][:, k, 0:n], func=AF.Identity,
                                                                       scale=gm, bias=sh),
                         reads=[("xb", i)] + MODR, writes=dst_res)

        def _ss_and_rstd_begin(c0, c1, src, sres):
            n = c1 - c0
            P.op("act", lambda e: e.activation(out=h_ext[:, :, c0:c1], in_=src, func=AF.Square),
                 reads=sres, writes=hgran(c0, c1))
            pb = bank()
            for kc in range(8):
                P.op("pe", lambda e, kc=kc: e.matmul(pbank(pb, n), lhsT=ones[:, :], rhs=h_ext[:, kc, c0:c1],
                                                     start=(kc == 0), stop=(kc == 7)),
                     reads=hgran(c0, c1) + ["ones"], writes=[("ps", pb)])
            return pb

        def _rstd_finish(c0, c1, pb):
            n = c1 - c0
            rr = ("rstd", c0 // 344)
            P.op("act", lambda e: e.activation(out=rstd[:, c0:c1], in_=pbank(pb, n), func=AF.Ln, bias=epsb[:, 0:1], scale=1.0),
                 reads=[("ps", pb), "epsb"], writes=[rr])
            P.op("act", lambda e: e.activation(out=rstd[:, c0:c1], in_=rstd[:, c0:c1], func=AF.Exp, scale=-0.5),
                 reads=[rr], writes=[rr])

        def norm_ext(xin, pt0, s):
            pbs = []
            for (c0, c1) in EXT_BLK:
                n = c1 - c0
                i = nb_rr[0] % 2
                nb_rr[0] += 1
                P.dma("sp", lambda e, i=i, c0=c0, c1=c1, n=n: e.dma_start(out=xblk[i][:, :, 0:n], in_=xin[:, :, pt0 + c0:pt0 + c1]),
                      writes=[("xb", i)])
                pbs.append(_ss_and_rstd_begin(c0, c1, xblk[i][:, :, 0:n], [("xb", i)]))
            for b, (c0, c1) in enumerate(EXT_BLK):
                _rstd_finish(c0, c1, pbs[b])
            for b, (c0, c1) in enumerate(EXT_BLK):
                n = c1 - c0
                i = nb_rr[0] % 2
                nb_rr[0] += 1
                P.dma("sp", lambda e, i=i, c0=c0, c1=c1, n=n: e.dma_start(out=xblk[i][:, :, 0:n], in_=xin[:, :, pt0 + c0:pt0 + c1]),
                      writes=[("xb", i)])
                P.op("dve", lambda e, i=i, c0=c0, c1=c1, n=n: e.tensor_tensor(
                    out=xblk[i][:, :, 0:n], in0=xblk[i][:, :, 0:n],
                    in1=rstd[:, c0:c1].unsqueeze(1).to_broadcast([128, 8, n]), op=ALU.mult),
                    reads=[("xb", i), ("rstd", b)], writes=[("xb", i)])
                _affine(i, n, lambda k: GM[:, 0, 0, s, k:k + 1], lambda k: modcol(0, k, s),
                        lambda k, c0=c0, c1=c1: h_ext[:, k, c0:c1], hgran(c0, c1))

        SHC = NRES

        def norm_res(blocks, gm_fn, sh_ap, kind, yout=None, pt0=0, tag=None):
            while ss_pend:
                ss_pend.pop(0)()
            for b in (2, 0, 1):
                f0, f1 = RES_BLK[b]
                _rstd_finish(f0, f1, SSB[b])
            if sh_ap is not None:
                P.op("dve", lambda e: e.tensor_copy(out=h_ext[:, :, SHC:SHC + 1], in_=sh_ap),
                     reads=MODR, writes=[("h", SHC // 128)])
            for b in (2, 0, 1):
                c0, c1 = blocks[b]
                n = c1 - c0
                if kind == "h":
                    for k in range(8):
                        P.op("dve", lambda e, k=k, c0=c0, c1=c1: e.scalar_tensor_tensor(
                            out=h_ext[:, k, c0:c1], in0=x_res[:, k, c0:c1], scalar=gm_fn(k), in1=rstd[:, c0:c1],
                            op0=ALU.mult, op1=ALU.mult),
                            reads=xres_of(c0, c1) + [("rstd", c0 // 344)] + MODR, writes=hgran(c0, c1))
                else:
                    i = nb_rr[0] % 2
                    nb_rr[0] += 1
                    for k in range(8):
                        P.op("dve", lambda e, k=k, c0=c0, c1=c1, i=i, n=n: e.scalar_tensor_tensor(
                            out=xblk[i][:, k, 0:n], in0=x_res[:, k, c0:c1], scalar=gm_fn(k), in1=rstd[:, c0:c1],
                            op0=ALU.mult, op1=ALU.mult),
                            reads=xres_of(c0, c1) + [("rstd", c0 // 344), "FG32"], writes=[("xb", i)])
                    P.dma("sp", lambda e, i=i, n=n, c0=c0: e.dma_start(
                        out=yout[:, :, pt0 + c0 - 3:pt0 + c0 - 3 + n], in_=xblk[i][:, :, 0:n]),
                        reads=[("xb", i)], writes=[("y", tag, c0)])

        bc_rr = [0]

        def dense_fm_bias(wt, wres, evac):
            bi = bc_rr[0] % 4
            bc_rr[0] += 1
            bcol = bcolb[bi]
            order = [RES_BLK[2], RES_BLK[0], RES_BLK[1]]
            for oi, (c0, c1) in enumerate(order):
                pb = bank()
                ce = c1 + 1 if oi == 0 else c1
                n = ce - c0
                for kc in range(8):
                    P.op("pe", lambda e, kc=kc, pb=pb, c0=c0, ce=ce, n=n: e.matmul(
                        pbank(pb, n), lhsT=wt[:, kc, :], rhs=h_ext[:, kc, c0:ce], start=(kc == 0), stop=(kc == 7)),
                        reads=[wres] + hgran(c0, ce), writes=[("ps", pb)])
                if oi == 0:
                    P.op("act", lambda e, pb=pb, n=n: e.activation(out=bcol[:, :], in_=ps[:, pb * 512 + n - 1:pb * 512 + n], func=AF.Copy),
                         reads=[("ps", pb)], writes=[("bcol", bi)])
                evac(pb, c0, c1, bcol, ("bcol", bi))

        SSB = (4, 5, 6)
        ss_pend = []
        DBANKS = (0, 1, 2, 3)

        def resid_evac(gate_ap, oc, pb, c0, c1):
            n = c1 - c0
            P.op("dve", lambda e: e.scalar_tensor_tensor(
                out=x_res[:, oc, c0:c1], in0=pbank(pb, n), scalar=gate_ap, in1=x_res[:, oc, c0:c1],
                op0=ALU.mult, op1=ALU.add), reads=[("ps", pb)] + MODR + xres_of(c0, c1), writes=xres_of(c0, c1))
            b = [i for i, (a0, a1) in enumerate(RES_BLK) if a0 <= c0 < a1][0]
            f0, f1 = RES_BLK[b]
            nf = f1 - f0
            P.op("act", lambda e: e.activation(out=h_ext[:, oc, f0:f1], in_=x_res[:, oc, f0:f1], func=AF.Square),
                 reads=xres_of(f0, f1), writes=hgran(f0, f1))
            ss_pend.append(lambda: P.op("pe", lambda e: e.matmul(pbank(SSB[b], nf), lhsT=ones[:, :], rhs=h_ext[:, oc, f0:f1],
                                                                 start=(oc == 0), stop=(oc == 7)),
                                        reads=hgran(f0, f1) + ["ones"], writes=[("ps", SSB[b])]))

        def dense_fm(wt, wres, rhs_fn, rhs_res_fn, blocks, nk, evac, banks=(0, 1, 2, 3, 4, 5, 6)):
            for (c0, c1) in blocks:
                while len(ss_pend) > 2:
                    ss_pend.pop(0)()
                pb = bank(banks)
                n = c1 - c0
                for kc in range(nk):
                    P.op("pe", lambda e, kc=kc, pb=pb, c0=c0, c1=c1, n=n: e.matmul(
                        pbank(pb, n), lhsT=wt[:, kc, :], rhs=rhs_fn(kc, c0, c1), start=(kc == 0), stop=(kc == nk - 1)),
                        reads=[wres] + rhs_res_fn(c0, c1), writes=[("ps", pb)])
                evac(pb, c0, c1)

        def xres_of(c0, c1):
            return [("x", b) for b, (a0, a1) in enumerate(RES_BLK) if a0 < c1 and c0 < a1]

        def tile_prog(seg, t):
            s = seg
            xin = xp if seg == 0 else xs
            yout = yp if seg == 0 else ys
            rows = P_ROWS if seg == 0 else S_ROWS
            pt0 = TT * t
            first = (t == 0)
            last = (t == (3 if seg == 0 else 1))
            if seg == 0:
                plans = [_pair_plan_static(TR * t, P_ROWS, j) for j in range(NPAIR)]
            else:
                plans = splans[t]

            fence()
            st["wd_ok"] = False
            for b, (c0, c1) in enumerate(RES_BLK):
                P.dma("sp", lambda e, c0=c0, c1=c1: e.dma_start(out=x_res[:, :, c0:c1],
                                                              in_=xin[:, :, pt0 + RES0 + c0:pt0 + RES0 + c1]),
                      writes=[("x", b)])
            P.dma("sp", lambda e: e.dma_start(out=BVB, in_=bvb_in[:, :]), reads=[RXL], writes=["bvb"])
            if seg == 1:
                P.dma("sp", lambda e: e.dma_start(out=MTS, in_=mtb[:, :, :]), reads=["mtb", RXL], writes=["mts"])
            norm_ext(xin, pt0, s)
            dump("h_ext_%d_%d" % (seg, t), h_ext[:, :, :], hgran(0, EXT))
            for vb in range(2):
                P.op("dve", lambda e, vb=vb: e.memset(VA[vb][:, :, :, 64:65], 1.0), reads=[RXL], writes=[("va", vb)])
            QBLK = [(QOFF, QOFF + 384), (QOFF + 384, QOFF + 768), (QOFF + 768, QOFF + 1152)]
            KBLK = [(0, 416), (416, 832), (832, 1248), (1248, 1664)]

            def qkv_groups(c, fixed_bank):
                qb = c % 2
                hold = {}
                gs = []

                def pick():
                    return fixed_bank if fixed_bank is not None else bank()

                for bi, (c0, c1) in enumerate(QBLK):
                    def g(bi=bi, c0=c0, c1=c1):
                        if bi == 0:
                            if tab_done[c]:
                                P.dma("sp", lambda e: e.dma_start(out=TAB[qb], in_=tabb[:, 2 * c:2 * c + 2, :]),
                                      reads=[("tabb", c), RXL], writes=[("tab", qb)])
                            else:
                                tab_done[c] = True
                                P.dma("sp", lambda e: e.dma_start(out=STG, in_=btab_in[:, 2 * c:2 * c + 2, :]),
                                      reads=[RXL], writes=["stg"])
                                P.op("act", lambda e: e.activation(out=TAB[qb], in_=STG, func=AF.Exp),
                                     reads=["stg", RXL], writes=[("tab", qb)])
                                P.dma("sp", lambda e: e.dma_start(out=tabb[:, 2 * c:2 * c + 2, :], in_=TAB[qb]),
                                      reads=[("tab", qb), RXL], writes=[("tabb", c)])
                            hold["q"] = wnext("w8", W_Q + c)
                        wt, wres = hold["q"]
                        pb = pick()
                        n = c1 - c0
                        for kc in range(8):
                            P.op("pe", lambda e, kc=kc: e.matmul(
                                pbank(pb, n), lhsT=wt[:, kc, :], rhs=h_ext[:, kc, c0:c1], start=(kc == 0), stop=(kc == 7)),
                                reads=[wres] + hgran(c0, c1), writes=[("ps", pb)])
                        P.op("act", lambda e: e.activation(
                            out=QT[qb][:, c0 - QOFF:c1 - QOFF], in_=pbank(pb, n), func=AF.Identity, bias=vcol(V_BQK + c), scale=1.0),
                            reads=[("ps", pb), "vecs", RXL], writes=[("qt", qb)])
                    gs.append(g)
                for bi, (c0, c1) in enumerate(KBLK):
                    def g(bi=bi, c0=c0, c1=c1):
                        if bi == 0:
                            hold["k"] = wnext("w8", W_K + c)
                        wt, wres = hold["k"]
                        pb = pick()
                        n = c1 - c0
                        for kc in range(8):
                            P.op("pe", lambda e, kc=kc: e.matmul(
                                pbank(pb, n), lhsT=wt[:, kc, :], rhs=h_ext[:, kc, c0:c1], start=(kc == 0), stop=(kc == 7)),
                                reads=[wres] + hgran(c0, c1), writes=[("ps", pb)])
                        P.op("act", lambda e: e.activation(
                            out=KT[qb][:, c0:c1], in_=pbank(pb, n), func=AF.Identity, bias=vcol(V_BQK + 8 + c), scale=1.0),
                            reads=[("ps", pb), "vecs", RXL], writes=[("kt", qb)])
                    gs.append(g)
                for m0 in range(0, NCH, 4):
                    def g(m0=m0):
                        if m0 == 0:
                            hold["v"] = wnext("w8", W_V + c)
                        wt, wres = hold["v"]
                        mm = min(4, NCH - m0)
                        pb = pick()
                        for mi in range(mm):
                            m = m0 + mi
                            for kc in range(8):
                                P.op("pe", lambda e, kc=kc, m=m, mi=mi: e.matmul(
                                    ps[:, pb * 512 + mi * 128:pb * 512 + (mi + 1) * 128], lhsT=h_ext[:, kc, m * 128:(m + 1) * 128],
                                    rhs=wt[:, kc, :], start=(kc == 0), stop=(kc == 7)),
                                    reads=[wres, ("h", m)], writes=[("ps", pb)])
                        P.op("dve", lambda e: e.tensor_tensor(
                            out=VA[qb][:, m0:m0 + mm, :, 0:64],
                            in0=ps[:, pb * 512:pb * 512 + mm * 128].rearrange("p (m h d) -> p m h d", h=2, d=64),
                            in1=BVB[:, c * 128:(c + 1) * 128].rearrange("p (h d) -> p h d", h=2).unsqueeze(1).to_broadcast([128, mm, 2, 64]),
                            op=ALU.add), reads=[("ps", pb), "bvb", RXL], writes=[("va", qb)])
                    gs.append(g)
                return gs

            def emit_qk_sm(c, j, hh, ui):
                qb = c % 2
                pl = plans[j]
                n = pl["n"]
                sbk = (ui % 2) * 2
                eb = ui % 2
                hp0, hp1 = hh * 64, hh * 64 + 64
                for i in range(n):
                    cc = pl["c_hi"] - i
                    P.op("pe", lambda e, i=i, cc=cc: e.matmul(
                        ps[:, sbk * 512 + i * 128:sbk * 512 + (i + 1) * 128],
                        lhsT=KT[qb][hp0:hp1, cc * 128:(cc + 1) * 128],
                        rhs=QT[qb][hp0:hp1, j * 128:(j + 1) * 128], start=True, stop=True),
                        reads=[("kt", qb), ("qt", qb)], writes=[("ps", sbk), ("ps", sbk + 1)])
                P.op("act", lambda e: e.activation(
                    out=EB[eb][:, 0:n * 128], in_=ps[:, sbk * 512:sbk * 512 + n * 128], func=AF.Exp, scale=0.125),
                    reads=[("ps", sbk), ("ps", sbk + 1), RXL], writes=[("eb", eb)])
                s0 = pl["s0"]
                P.op("dve", lambda e: e.tensor_tensor(
                    out=PB[eb][:, 0:n * 128], in0=EB[eb][:, 0:n * 128],
                    in1=TAB[qb][:, hh, s0 * 64:s0 * 64 + n * 128], op=ALU.mult),
                    reads=[("eb", eb), ("tab", qb)], writes=[("pb", eb)])
                if pl["dyn"] is not None:
                    dy = pl["dyn"]
                    P.op("dve", lambda e: e.tensor_tensor(
                        out=PB[eb][:, 0:n * 128], in0=PB[eb][:, 0:n * 128], in1=MTS[:, dy, 0:n * 128], op=ALU.mult),
                        reads=[("pb", eb), "mts"], writes=[("pb", eb)])

            def emit_pv(c, j, hh, ui):
                qb = c % 2
                pl = plans[j]
                n = pl["n"]
                eb = ui % 2
                ob = 4 + (j % 2)
                if pl["dyn"] is not None:
                    for i in range(n):
                        cc = pl["c_hi"] - i
                        P.op("pe", lambda e, i=i, cc=cc: e.matmul(
                            ps[:, ob * 512 + hh * 65:ob * 512 + hh * 65 + 65],
                            lhsT=PB[eb][:, i * 128:(i + 1) * 128], rhs=VA[qb][:, cc, hh, :],
                            start=(i == 0), stop=(i == n - 1)),
                            reads=[("pb", eb), ("va", qb)], writes=[("ps", ob)])
                else:
                    for qi in range(2):
                        lst = pl["pv"][qi]
                        for li, (i, mode) in enumerate(lst):
                            cc = pl["c_hi"] - i
                            k0, k1 = {"both": (0, 128), "top": (0, 64), "bot": (64, 128)}[mode]
                            P.op("pe", lambda e, i=i, cc=cc, qi=qi, k0=k0, k1=k1, li=li, nl=len(lst): e.matmul(
                                ps[qi * 64:(qi + 1) * 64, ob * 512 + hh * 65:ob * 512 + hh * 65 + 65],
                                lhsT=PB[eb][k0:k1, i * 128 + qi * 64:i * 128 + qi * 64 + 64],
                                rhs=VA[qb][k0:k1, cc, hh, :], start=(li == 0), stop=(li == nl - 1)),
                                reads=[("pb", eb), ("va", qb)], writes=[("ps", ob)])
                if hh == 1:
                    rb = j % 2
                    ov = ps[:, ob * 512:ob * 512 + 130].rearrange("p (h d) -> p h d", d=65)
                    P.op("dve", lambda e: e.reciprocal(out=rec[rb][:, :], in_=ov[:, :, 64]),
                         reads=[("ps", ob)], writes=[("rec", rb)])
                    P.op("dve", lambda e: e.tensor_tensor(
                        out=OSB[:, j, c * 128:(c + 1) * 128].rearrange("p (h d) -> p h d", d=64),
                        in0=ov[:, :, 0:64], in1=rec[rb][:, :].unsqueeze(2).to_broadcast([128, 2, 64]), op=ALU.mult),
                        reads=[("ps", ob), ("rec", rb), RXL], writes=[("osb", j)])

            for g in qkv_groups(0, None):
                g()
            units = [(j, hh) for j in range(NPAIR) for hh in range(2)]
            for c in range(8):
                nxt = qkv_groups(c + 1, 6) if c < 7 else []
                for ui, (j, hh) in enumerate(units):
                    emit_qk_sm(c, j, hh, ui)
                    if ui >= 1:
                        emit_pv(c, units[ui - 1][0], units[ui - 1][1], ui - 1)
                    if nxt:
                        nxt.pop(0)()
                    if ada_rest:
                        ada_rest.pop(0)(6)
                emit_pv(c, units[-1][0], units[-1][1], len(units) - 1)
                while nxt:
                    nxt.pop(0)()
            while ada_rest:
                ada_rest.pop(0)(6)
            for j in range(NPAIR):
                for oc in range(8):
                    P.op("pe", lambda e, j=j, oc=oc: e.transpose(out=psT[:, oc * 128:(oc + 1) * 128],
                                                                 in_=OSB[:, j, oc * 128:(oc + 1) * 128], identity=ident[:, :]),
                         reads=[("osb", j), "ident"], writes=["psT"])
                P.op("act", lambda e, j=j: e.activation(out=OT[:, :, j * 128:(j + 1) * 128],
                                                        in_=psT[:, :].rearrange("p (k q) -> p k q", k=8), func=AF.Copy),
                     reads=["psT", RXL], writes=[("ot", j)])

            dump("OT_%d_%d" % (seg, t), OT, [("ot", jj) for jj in range(NPAIR)])
            dump("OSB_%d_%d" % (seg, t), OSB, [("osb", jj) for jj in range(NPAIR)])

            def otres(c0, c1):
                a, b = c0 + RES0 - QOFF, c1 + RES0 - QOFF
                return [("ot", jj) for jj in range(a // 128, (b + 127) // 128)]

            for oc in range(8):
                wt, wres = wnext("w8", W_O + oc)

                def evac(pb, c0, c1, oc=oc):
                    resid_evac(modcol(0, 16 + oc, s), oc, pb, c0, c1)
                dense_fm(wt, wres, lambda kc, c0, c1: OT[:, kc, c0 + RES0 - QOFF:c1 + RES0 - QOFF],
                         lambda c0, c1: otres(c0, c1) + [RXL], RES_BLK, 8, evac, banks=DBANKS)

            dump("x0m_%d_%d" % (seg, t), x_res[:, :, :], xres_of(0, NRES))
            fence()
            st["wd_ok"] = True
            ffn(0, s, seg, first, last)
            dump("x1_%d_%d" % (seg, t), x_res[:, :, :], xres_of(0, NRES))
            sc_mixer(s, seg, first, last)
            dump("x1m_%d_%d" % (seg, t), x_res[:, :, :], xres_of(0, NRES))
            ffn(1, s, seg, first, last)
            dump("x2_%d_%d" % (seg, t), x_res[:, :, :], xres_of(0, NRES))
            norm_res([(3, 344), (344, 688), (688, 1027)], lambda k: FG32[:, k:k + 1], None, "y",
                     yout=yout, pt0=pt0, tag=(seg, t))

        def pre_norm(l, which, s):
            base = 0 if which == 0 else 24
            norm_res(RES_BLK, lambda k: GM[:, l, which, s, k:k + 1], MOD[:, l, base:base + 8, s:s + 1], "h")

        def edge_fix(buf, res, seg, first, last):
            if seg == 0:
                if first:
                    P.op("dve", lambda e: e.memset(buf[:, 2:3], 0.0), reads=[], writes=[res])
                if last:
                    P.op("dve", lambda e: e.memset(buf[:, 1027:1028], 0.0), reads=[], writes=[res])
            else:
                if first:
                    P.op("dve", lambda e: e.tensor_scalar(out=buf[:, 2:3], in0=buf[:, 2:3], scalar1=pcs[:, 0:1], scalar2=None,
                                                          op0=ALU.mult), reads=[res, "pcs"], writes=[res])
                if last:
                    P.op("dve", lambda e: e.tensor_scalar(out=buf[:, 1027:1028], in0=buf[:, 1027:1028], scalar1=pcs[:, 1:2],
                                                          scalar2=None, op0=ALU.mult), reads=[res, "pcs"], writes=[res])

        def conv3(src, dst, w0, w1, w2, rs, rd):
            P.op("dve", lambda e: e.tensor_scalar(out=dst[:, 1:NRES - 1], in0=src[:, 1:NRES - 1], scalar1=w1, scalar2=None,
                                                  op0=ALU.mult), reads=[rs, "vecs"], writes=[rd])
            P.op("dve", lambda e: e.scalar_tensor_tensor(out=dst[:, 1:NRES - 1], in0=src[:, 0:NRES - 2], scalar=w0,
                                                         in1=dst[:, 1:NRES - 1], op0=ALU.mult, op1=ALU.add),
                 reads=[rs, rd, "vecs"], writes=[rd])
            P.op("dve", lambda e: e.scalar_tensor_tensor(out=dst[:, 1:NRES - 1], in0=src[:, 2:NRES], scalar=w2,
                                                         in1=dst[:, 1:NRES - 1], op0=ALU.mult, op1=ALU.add),
                 reads=[rs, rd, "vecs"], writes=[rd])

        def ffn(l, s, seg, first, last):
            pre_norm(l, 1, s)
            if l == 0:
                dump("h2", h_ext[:, :, 0:NRES], hgran(0, NRES))
            wup = W_UP0 if l == 0 else W_UP1
            for j in range(NJ):
                par = j % 2
                A, Vv, Tt = TMP[par]
                rA, rV, rT = ("tA", par), ("tV", par), ("tT", par)
                for (widx, dst, rdst) in ((wup + j, A, rA), (wup + NJ + j, Vv, rV)):
                    wt, wres = wnext("w8", widx)

                    def evac(pb, c0, c1, bcol, bres, dst=dst, rdst=rdst):
                        n = c1 - c0
                        P.op("act", lambda e: e.activation(out=dst[:, c0:c1], in_=pbank(pb, n), func=AF.Identity,
                                                           bias=bcol[:, 0:1], scale=1.0),
                             reads=[("ps", pb), bres, RXL], writes=[rdst])
                    dense_fm_bias(wt, wres, evac)
                edge_fix(A, rA, seg, first, last)
                if l == 0 and j == 0:
                    dump("ffn_A", A[:, 0:NRES], [rA])
                    dump("ffn_V", Vv[:, 0:NRES], [rV])
                cw = V_FCW + l * 66
                conv3(A, Tt, vcol(cw + j), vcol(cw + 22 + j), vcol(cw + 44 + j), rA, rT)
                P.op("act", lambda e, Tt=Tt, l=l, j=j: e.activation(out=Tt[:, 1:NRES - 1], in_=Tt[:, 1:NRES - 1], func=AF.Silu,
                                                                   bias=vcol(V_FCB + l * 22 + j), scale=1.0),
                     reads=[rT, "vecs", RXL], writes=[rT])
                P.op("dve", lambda e, Tt=Tt, Vv=Vv, j=j: e.tensor_tensor(out=GB[:, j, 1:NRES - 1], in0=Tt[:, 1:NRES - 1],
                                                                         in1=Vv[:, 1:NRES - 1], op=ALU.mult),
                     reads=[rT, rV, RXL], writes=[("g", j)])
                if l == 0 and j == 0:
                    dump("ffn_T", Tt[:, 0:NRES], [rT])
                    dump("ffn_G", GB[:, 0, :], [("g", 0)])
            DBLK = [(1, 344), (344, 688), (688, 1029)]
            for oc in range(8):
                wt, wres = wnext("wd", l * 8 + oc)

                def evac(pb, c0, c1, oc=oc):
                    resid_evac(modcol(l, 40 + oc, s), oc, pb, c0, c1)
                dense_fm(wt, wres, lambda kc, c0, c1: GB[:, kc, c0:c1], lambda c0, c1: [("g", jj) for jj in range(NJ)] + [RXL],
                         DBLK, NJ, evac, banks=DBANKS)

        def sc_mixer(s, seg, first, last):
            pre_norm(1, 0, s)
            for k in range(8):
                par = k % 2
                A, Vv, Tt = TMP[par]
                rA, rV, rT = ("tA", par), ("tV", par), ("tT", par)
                for (widx, dst, rdst) in ((W_IN + 8 + k, A, rA), (W_IN + 16 + k, Vv, rV)):
                    wt, wres = wnext("w8", widx)

                    def evac(pb, c0, c1, bcol, bres, dst=dst, rdst=rdst):
                        n = c1 - c0
                        P.op("act", lambda e: e.activation(out=dst[:, c0:c1], in_=pbank(pb, n), func=AF.Identity,
                                                           bias=bcol[:, 0:1], scale=1.0),
                             reads=[("ps", pb), bres, RXL], writes=[rdst])
                    dense_fm_bias(wt, wres, evac)
                P.op("dve", lambda e, A=A, Vv=Vv: e.tensor_tensor(out=A[:, 0:NRES], in0=A[:, 0:NRES], in1=Vv[:, 0:NRES], op=ALU.mult),
                     reads=[rA, rV], writes=[rA])
                edge_fix(A, rA, seg, first, last)
                conv3(A, Tt, vcol(V_SCW + k), vcol(V_SCW + 8 + k), vcol(V_SCW + 16 + k), rA, rT)
                def evac_b(pb, c0, c1, bcol, bres, Vv=Vv, rV=rV):
                    n = c1 - c0
                    P.op("act", lambda e: e.activation(out=Vv[:, c0:c1], in_=pbank(pb, n), func=AF.Identity,
                                                       bias=bcol[:, 0:1], scale=1.0),
                         reads=[("ps", pb), bres, RXL], writes=[rV])
                wtb, wrb = wnext("w8", W_IN + k)
                dense_fm_bias(wtb, wrb, evac_b)
                P.op("dve", lambda e, Tt=Tt, Vv=Vv, k=k: e.tensor_tensor(out=ZB[:, k, 1:NRES - 1], in0=Tt[:, 1:NRES - 1],
                                                                         in1=Vv[:, 1:NRES - 1], op=ALU.mult),
                     reads=[rT, rV, RXL], writes=[("z", k)])
            DBLK = [(1, 344), (344, 688), (688, 1029)]
            for oc in range(8):
                wt, wres = wnext("w8", W_OUT + oc)

                def evac(pb, c0, c1, oc=oc):
                    resid_evac(modcol(1, 16 + oc, s), oc, pb, c0, c1)
                dense_fm(wt, wres, lambda kc, c0, c1: ZB[:, kc, c0:c1], lambda c0, c1: [("z", kk) for kk in range(8)] + [RXL],
                         DBLK, 8, evac, banks=DBANKS)

        for (seg, t) in tiles:
            tile_prog(seg, t)
        P.emit()
    return nc, P


def _fm(x):
    T = x.shape[0]
    return np.ascontiguousarray(x.reshape(T, 8, 128).transpose(2, 1, 0))


def _wchunks(W):
    K, N = W.shape
    a = W.reshape(K // 128, 128, N // 128, 128).transpose(2, 1, 0, 3)
    return np.ascontiguousarray(a).reshape(N // 128, 128, (K // 128) * 128)


def _pvec(v):
    return np.ascontiguousarray(v.reshape(-1, 128).T)


def _bias_table(rpb):
    kc = np.arange(64)[:, None]
    qc = np.arange(64)[None, :]
    cs = np.clip(qc - 8, 0, 48)
    cvalid = (kc >= cs) & (kc < cs + 16)
    dc = np.clip(kc - qc + 15, 0, 30)
    out = np.full((128, NH, NSLOT, 64), NEG, np.float32)
    for s in range(NSLOT):
        for half, delta in ((0, 8 - s), (1, 9 - s)):
            if -7 <= delta <= 7:
                g = rpb[:, delta + 7][:, dc]
                g = np.where(cvalid[None], g, NEG).astype(np.float32)
                out[half * 64:(half + 1) * 64, :, s, :] = g.transpose(1, 0, 2)
    return out.reshape(128, NH, NSLOT * 64)


_CACHE = {}


def kernel(x_prompt, x_sample, c_prompt, c_sample, ada_w, ada_b, norm1_g, norm2_g,
           na_w_qkv, na_b_qkv, na_rpb, na_w_o, sc_w_in, sc_conv_w, sc_w_out,
           ffn_w_up, ffn_conv_w, ffn_conv_b, ffn_w_down, final_g, _tiles=None, _debug=False):
    f = lambda a: np.asarray(a, dtype=np.float32)
    x_prompt, x_sample, c_prompt, c_sample = f(x_prompt), f(x_sample), f(c_prompt), f(c_sample)
    key = (None if _tiles is None else tuple(_tiles), _debug)
    if key not in _CACHE:
        _CACHE[key] = build_program(_tiles, _debug)
    nc, _ = _CACHE[key]
    _, masks = _sample_plans()

    w8 = np.concatenate([
        _wchunks(f(na_w_qkv)[0]),
        _wchunks(f(na_w_o)[0]),
        _wchunks(f(sc_w_in)[0]),
        _wchunks(f(sc_w_out)[0]),
        _wchunks(f(ffn_w_up)[0]),
        _wchunks(f(ffn_w_up)[1]),
    ], 0)
    assert w8.shape[0] == NW8
    wd = np.stack([np.ascontiguousarray(
        f(ffn_w_down)[l].reshape(NJ, 128, 8, 128).transpose(2, 1, 0, 3)).reshape(8, 128, NJ * 128)
        for l in range(2)], 0).reshape(16, 128, NJ * 128)
    ada = np.concatenate([_wchunks(f(ada_w)[l]) for l in range(2)], 0)
    vecs = np.zeros((128, NV), np.float32)
    for l in range(2):
        vecs[:, V_ADAB + l * 48:V_ADAB + (l + 1) * 48] = _pvec(f(ada_b)[l])
        vecs[:, V_N1 + l * 8:V_N1 + (l + 1) * 8] = _pvec(f(norm1_g)[l])
        vecs[:, V_N2 + l * 8:V_N2 + (l + 1) * 8] = _pvec(f(norm2_g)[l])
        for tap in range(3):
            vecs[:, V_FCW + l * 66 + tap * 22:V_FCW + l * 66 + (tap + 1) * 22] = _pvec(f(ffn_conv_w)[l, tap])
        vecs[:, V_FCB + l * 22:V_FCB + (l + 1) * 22] = _pvec(f(ffn_conv_b)[l])
    vecs[:, V_FG:V_FG + 8] = _pvec(f(final_g))
    vecs[:, V_BQK:V_BQK + 16] = _pvec(f(na_b_qkv)[0, :2048])
    for tap in range(3):
        vecs[:, V_SCW + tap * 8:V_SCW + (tap + 1) * 8] = _pvec(f(sc_conv_w)[0, tap])
    bvb = np.ascontiguousarray(np.broadcast_to(f(na_b_qkv)[0, 2048:][None, :], (128, D)))
    ident = np.eye(128, dtype=np.float32)
    btab = _bias_table(f(na_rpb)[0])

    xs_pad = np.zeros((S_ROWS * GW + 2 * HR * GW, D), np.float32)
    xs_pad[HR * GW:HR * GW + S_ROWS * GW] = x_sample[0]
    in_maps = []
    for c in range(8):
        xpp = np.zeros((P_ROWS * GW + 2 * HR * GW, D), np.float32)
        xpp[HR * GW:HR * GW + P_ROWS * GW] = x_prompt[c]
        xsc = xs_pad[c * S_LOC * GW:c * S_LOC * GW + S_LOC * GW + 2 * HR * GW]
        cvv = np.stack([_pvec(c_prompt[c]), _pvec(c_sample[0])], -1)
        pcm = np.zeros((128, 4), np.float32)
        pcm[:, 0] = 0.0 if c == 0 else 1.0
        pcm[:, 1] = 0.0 if c == 7 else 1.0
        mt = np.stack([m[c] for m in masks], 1)
        in_maps.append({
            "xp": _fm(xpp), "xs": _fm(xsc), "cv": np.ascontiguousarray(cvv), "pcm": pcm, "mt": np.ascontiguousarray(mt),
            "vecs": vecs, "bvb": bvb, "ident": ident, "btab": btab, "ada": ada, "w8": w8, "wd": wd,
        })
    res = run_bass_kernel_spmd(nc, in_maps, core_ids=list(range(8)))
    y_p = np.empty((8, P_ROWS * GW, D), np.float32)
    y_s = np.empty((1, S_ROWS * GW, D), np.float32)
    for c in range(8):
        r = res.results[c]
        y_p[c] = np.asarray(r["yp"]).transpose(2, 1, 0).reshape(P_ROWS * GW, D)
        y_s[0, c * S_LOC * GW:(c + 1) * S_LOC * GW] = np.asarray(r["ys"]).transpose(2, 1, 0).reshape(S_LOC * GW, D)
    if _debug:
        return (y_p, y_s), res.results
    return (y_p, y_s)
```

```python
import numpy as np
import ml_dtypes
from contextlib import ExitStack
import concourse.bass as bass
import concourse.mybir as mybir
from concourse.bass_utils import run_bass_kernel_spmd

F32 = mybir.dt.float32
BF16 = mybir.dt.bfloat16
AF = mybir.ActivationFunctionType
ALU = mybir.AluOpType

D = 1024
NH = 16
FF = 2816
NJ = FF // 128
GW = 64
EPS = 1e-6
NEG = -30000.0

TR = 16
TT = TR * GW
HR = 5
EXT = (TR + 2 * HR) * GW
NCH = EXT // 128
QOFF = 256
NQ = 1152
NPAIR = 9
RES0 = 317
NRES = TT + 6
P_ROWS = 64
S_ROWS = 256
S_LOC = 32
NSLOT = 18
NB_EXT = 208
RES_BLK = [(0, 344), (344, 688), (688, 1030)]
NRM_BLK = [(0, 206), (206, 412), (412, 618), (618, 824), (824, 1030)]

W_Q, W_K, W_V, W_O, W_IN, W_OUT, W_UP0, W_UP1, NW8 = 0, 8, 16, 24, 32, 56, 64, 108, 152

V_ADAB = 0
V_N1 = 96
V_N2 = 112
V_FG = 128
V_BQK = 136
V_SCW = 152
V_FCW = 176
V_FCB = 308
NV = 352

COMPUTE_Q = ("pe", "act", "dve", "pool")
N_DMA_SEMS = 24


class Prog:
    def __init__(self, nc):
        self.nc = nc
        self.q = {k: [] for k in ("pe", "act", "dve", "pool", "sp")}
        self.res = {}
        self.dma_cnt = [0] * N_DMA_SEMS
        self.dma_rr = 0

    def _collect(self, reads, writes, me_q, is_dma):
        waits = set()
        for r in reads:
            st = self.res.get(r)
            if st is not None and st["w"] is not None:
                waits.add(st["w"])
        for wname in writes:
            st = self.res.get(wname)
            if st is None:
                continue
            w = st["w"]
            if w is not None and (is_dma or not (w[0] == "c" and w[1] == me_q)):
                waits.add(w)
            for qn, idx in st["rc"].items():
                if is_dma or qn != me_q:
                    waits.add(("c", qn, idx))
            for d in st["rd"]:
                waits.add(d)
        if not is_dma and me_q == "pe":
            waits = {w for w in waits if not (w[0] == "c" and w[1] == "pe")}
        return waits

    def _record(self, reads, writes, me):
        for r in reads:
            st = self.res.setdefault(r, {"w": None, "rc": {}, "rd": []})
            if me[0] == "c":
                st["rc"][me[1]] = me[2]
            else:
                st["rd"].append(me)
        for wname in writes:
            self.res[wname] = {"w": me, "rc": {}, "rd": []}

    def op(self, q, fn, reads=(), writes=()):
        waits = self._collect(reads, writes, q, False)
        idx = len(self.q[q])
        self.q[q].append({"fn": fn, "waits": waits, "inc": False, "dma": None})
        self._record(reads, writes, ("c", q, idx))
        return idx

    def dma(self, q, fn, reads=(), writes=()):
        waits = self._collect(reads, writes, q, True)
        k = self.dma_rr
        self.dma_rr = (self.dma_rr + 1) % N_DMA_SEMS
        if self.dma_cnt[k] > 0:
            waits.add(("d", k, 16 * self.dma_cnt[k]))
        self.dma_cnt[k] += 1
        me = ("d", k, 16 * self.dma_cnt[k])
        self.q[q].append({"fn": fn, "waits": waits, "inc": False, "dma": k})
        self._record(reads, writes, me)
        return me

    def emit(self):
        nc = self.nc
        for qn, ops in self.q.items():
            for o in ops:
                for w in o["waits"]:
                    if w[0] == "c":
                        self.q[w[1]][w[2]]["inc"] = True
        val = {}
        for qn, ops in self.q.items():
            c = 0
            for i, o in enumerate(ops):
                if o["dma"] is None and o["inc"]:
                    c += 1
                    val[(qn, i)] = c
        with ExitStack() as es:
            csem = {qn: es.enter_context(nc.semaphore("s_" + qn)) for qn in COMPUTE_Q}
            dsem = [es.enter_context(nc.semaphore("d%d" % i)) for i in range(N_DMA_SEMS)]
            block = es.enter_context(nc.Block())

            def run(qn, eng):
                seen_c = {}
                seen_d = {}
                for o in self.q[qn]:
                    for w in sorted(o["waits"]):
                        if w[0] == "c":
                            v = val[(w[1], w[2])]
                            if seen_c.get(w[1], 0) >= v:
                                continue
                            seen_c[w[1]] = v
                            eng.wait_ge(csem[w[1]], v)
                        else:
                            if seen_d.get(w[1], 0) >= w[2]:
                                continue
                            seen_d[w[1]] = w[2]
                            eng.wait_ge(dsem[w[1]], w[2])
                    ins = o["fn"](eng)
                    if o["dma"] is not None:
                        ins.then_inc(dsem[o["dma"]], 16)
                    elif o["inc"]:
                        ins.then_inc(csem[qn], 1)
                if qn == "sp":
                    for k in range(N_DMA_SEMS):
                        if self.dma_cnt[k] > 0:
                            eng.wait_ge(dsem[k], 16 * self.dma_cnt[k])

            @block.tensor
            def _(e):
                run("pe", e)

            @block.scalar
            def _(e):
                run("act", e)

            @block.vector
            def _(e):
                run("dve", e)

            @block.gpsimd
            def _(e):
                run("pool", e)

            @block.sync
            def _(e):
                run("sp", e)


def _win(r, rows):
    r = min(max(r, 0), rows - 1)
    rs = min(max(r - 4, 0), rows - 8)
    return rs, rs + 8


def _pair_valid(R0, rows, j):
    rA = R0 - 1 + 2 * j
    out = {}
    for c in range(NCH):
        k0 = R0 - 5 + 2 * c
        v = [[False, False], [False, False]]
        for qi, r in enumerate((rA, rA + 1)):
            lo, hi = _win(r, rows)
            for ki, k in enumerate((k0, k0 + 1)):
                v[qi][ki] = lo <= k < hi
        out[c] = v
    return out


def _pair_plan_static(R0, rows, j):
    val = _pair_valid(R0, rows, j)
    used = [c for c in range(NCH) if any(val[c][0]) or any(val[c][1])]
    c_lo, c_hi = min(used), max(used)
    n = c_hi - c_lo + 1
    s0 = 12 - 2 * (c_hi - j)
    assert 0 <= s0 and s0 + 2 * n <= NSLOT, (R0, rows, j, s0, n)
    rows_pv = [[], []]
    for i in range(n):
        c = c_hi - i
        for qi in range(2):
            t, b = val[c][qi]
            if t and b:
                rows_pv[qi].append((i, "both"))
            elif t:
                rows_pv[qi].append((i, "top"))
            elif b:
                rows_pv[qi].append((i, "bot"))
    return dict(c_hi=c_hi, n=n, s0=s0, pv=rows_pv, dyn=None)


def _sample_plans():
    plans = []
    masks = []
    for t in range(2):
        tp = []
        for j in range(NPAIR):
            per_core = [_pair_valid(S_LOC * c + TR * t, S_ROWS, j) for c in range(8)]
            same = all(per_core[c] == per_core[1] for c in range(8))
            if same:
                tp.append(_pair_plan_static(S_LOC * 1 + TR * t, S_ROWS, j))
                continue
            used = sorted({cc for pc in per_core for cc in range(NCH)
                           if any(pc[cc][0]) or any(pc[cc][1])})
            c_lo, c_hi = used[0], used[-1]
            n = c_hi - c_lo + 1
            assert n <= 7
            s0 = 12 - 2 * (c_hi - j)
            assert 0 <= s0 and s0 + 2 * n <= NSLOT
            m = np.zeros((8, 128, 896), np.float32)
            for c in range(8):
                for i in range(n):
                    cc = c_hi - i
                    for qi in range(2):
                        for ki in range(2):
                            if per_core[c][cc][qi][ki]:
                                m[c, ki * 64:(ki + 1) * 64, i * 128 + qi * 64:i * 128 + qi * 64 + 64] = 1.0
            tp.append(dict(c_hi=c_hi, n=n, s0=s0, pv=None, dyn=len(masks)))
            masks.append(m)
        plans.append(tp)
    return plans, masks


def build_program(tiles=None, debug=False):
    if tiles is None:
        tiles = [(0, t) for t in range(4)] + [(1, t) for t in range(2)]
    nc = bass.Bass("TRN2", target_bir_lowering=False)
    dr = {}
    splans, _masks = _sample_plans()
    n_dyn = len(_masks)

    def din(name, shape, dt=F32):
        dr[name] = nc.dram_tensor(name, list(shape), dt, kind="ExternalInput").ap()
        return dr[name]

    xp = din("xp", [128, 8, P_ROWS * GW + 2 * HR * GW])
    xs = din("xs", [128, 8, S_LOC * GW + 2 * HR * GW])
    cv = din("cv", [128, 8, 2])
    pcm = din("pcm", [128, 4])
    mt_in = din("mt", [128, n_dyn, 896])
    vecs_in = din("vecs", [128, NV])
    bvb_in = din("bvb", [128, D])
    ident_in = din("ident", [128, 128])
    btab_in = din("btab", [128, NH, NSLOT * 64])
    ada_in = din("ada", [96, 128, 1024])
    w8_in = din("w8", [NW8, 128, 1024])
    wd_in = din("wd", [16, 128, NJ * 128])
    yp = nc.dram_tensor("yp", [128, 8, P_ROWS * GW], F32, kind="ExternalOutput").ap()
    ys = nc.dram_tensor("ys", [128, 8, S_LOC * GW], F32, kind="ExternalOutput").ap()
    w8b = nc.dram_tensor("w8b", [NW8, 128, 1024], BF16, kind="Internal").ap()
    wdb = nc.dram_tensor("wdb", [16, 128, NJ * 128], BF16, kind="Internal").ap()
    tabb = nc.dram_tensor("tabb", [128, NH, NSLOT * 64], BF16, kind="Internal").ap()
    mtb = nc.dram_tensor("mtb", [128, n_dyn, 896], BF16, kind="Internal").ap()

    P = Prog(nc)

    with ExitStack() as es:
        def sb(name, shape, dt):
            return es.enter_context(nc.sbuf_tensor("sb_" + name, list(shape), dt))

        x_res = sb("x_res", [128, 8, NRES], F32)
        h_ext = sb("h_ext", [128, 8, EXT + 2], BF16)
        xblk = [sb("xblk%d" % i, [128, 8, 344], F32) for i in range(2)]
        rstd = sb("rstd", [128, EXT], F32)
        epsb = sb("epsb", [128, 1], F32)
        w8s = [sb("w8s%d" % i, [128, 8, 128], BF16) for i in range(5)]
        vecs = sb("vecs", [128, NV], F32)
        ident = sb("ident", [128, 128], BF16)
        ones = sb("ones", [128, 128], BF16)
        pcs = sb("pcs", [128, 4], F32)
        MOD = sb("MOD", [128, 2, 48, 2], F32)
        GM = sb("GM", [128, 2, 2, 2, 8], F32)
        G32 = sb("G32", [128, 3, 8], F32)
        FG32 = sb("FG32", [128, 8], F32)
        csil = sb("csil", [128, 8, 2], F32)
        fscr = sb("fscr", [128, 4], F32)
        rec = [sb("rec%d" % i, [128, 2], F32) for i in range(2)]
        bcolb = [sb("bcol%d" % i, [128, 1], F32) for i in range(4)]
        RX_ELEMS = 49000
        RX = sb("RX", [128, RX_ELEMS], BF16)
        ps = es.enter_context(nc.psum_tensor("ps", [128, 7 * 512], F32))
        psT = es.enter_context(nc.psum_tensor("psT", [128, 1024], BF16))

        off = [0]

        def carve(n_elems_bf16):
            o = off[0]
            off[0] += n_elems_bf16
            assert off[0] <= RX_ELEMS, off[0]
            return o

        def v_bf(o, n):
            return RX[:, o:o + n]

        def v_f32(o, n):
            return RX[:, o:o + 2 * n].bitcast(F32)

        offA_QT = [carve(NQ) for _ in range(2)]
        offA_KT = [carve(EXT) for _ in range(2)]
        offA_V = [carve(NCH * 130) for _ in range(2)]
        offA_OSB = carve(NPAIR * D)
        offA_TAB = [carve(2 * NSLOT * 64) for _ in range(2)]
        offA_E = [carve(896) for _ in range(2)]
        offA_P = [carve(896) for _ in range(2)]
        offA_OT = carve(8 * NQ)
        offA_MTS = carve(n_dyn * 896)
        offA_BVB = carve(2 * D)
        offA_STG = carve(2 * 2 * NSLOT * 64)
        endA = off[0]
        off[0] = 0
        offB_G = carve(NJ * NRES)
        offB_Z = carve(8 * NRES)
        offB_T = [[carve(2 * NRES + 8) for _ in range(3)] for _ in range(2)]
        offB_WD = [carve(NJ * 128) for _ in range(2)]
        endB = off[0]
        off[0] = 0

        QT = [v_bf(o, NQ) for o in offA_QT]
        KT = [v_bf(o, EXT) for o in offA_KT]
        VA = [v_bf(o, NCH * 130).rearrange("p (m h d) -> p m h d", h=2, d=65) for o in offA_V]
        OSB = v_bf(offA_OSB, NPAIR * D).rearrange("p (j f) -> p j f", f=D)
        TAB = [v_bf(o, 2 * NSLOT * 64).rearrange("p (h s) -> p h s", h=2) for o in offA_TAB]
        EB = [v_bf(o, 896) for o in offA_E]
        PB = [v_bf(o, 896) for o in offA_P]
        OT = v_bf(offA_OT, 8 * NQ).rearrange("p (k q) -> p k q", k=8)
        MTS = v_bf(offA_MTS, n_dyn * 896).rearrange("p (a b) -> p a b", a=n_dyn)
        BVB = v_f32(offA_BVB, D)
        STG = v_f32(offA_STG, 2 * NSLOT * 64).rearrange("p (h s) -> p h s", h=2)
        GB = v_bf(offB_G, NJ * NRES).rearrange("p (j n) -> p j n", j=NJ)
        ZB = v_bf(offB_Z, 8 * NRES).rearrange("p (k n) -> p k n", k=8)
        TMP = [[v_f32(o, NRES + 4) for o in oo] for oo in offB_T]
        WDS = [v_bf(o, NJ * 128).rearrange("p (j c) -> p j c", j=NJ) for o in offB_WD]

        RXL = "RX"
        dbg_outs = {}

        def dump(name, ap, reads):
            if not debug:
                return
            t = nc.dram_tensor("dbg_" + name, list(ap.shape), ap.dtype, kind="ExternalOutput").ap()
            dbg_outs[name] = t
            P.dma("sp", lambda e: e.dma_start(out=t, in_=ap), reads=list(reads) + [RXL], writes=[("dbg", name)])

        def fence():
            P.op("dve", lambda e: e.memset(fscr[:, 0:1], 0.0), reads=[], writes=[RXL])

        def vcol(c):
            return vecs[:, c:c + 1]

        bank_rr = [0]

        def bank(banks=(0, 1, 2, 3, 4, 5, 6)):
            b = banks[bank_rr[0] % len(banks)]
            bank_rr[0] += 1
            return b

        def pbank(b, n=512, p0=0, p1=128):
            return ps[p0:p1, b * 512:b * 512 + n]

        stream = []
        for (seg, t) in tiles:
            for c in range(8):
                stream += [("w8", W_Q + c), ("w8", W_K + c), ("w8", W_V + c)]
            stream += [("w8", W_O + o) for o in range(8)]
            for j in range(NJ):
                stream += [("w8", W_UP0 + j), ("w8", W_UP0 + NJ + j)]
            stream += [("wd", o) for o in range(8)]
            for k in range(8):
                stream += [("w8", W_IN + 8 + k), ("w8", W_IN + 16 + k), ("w8", W_IN + k)]
            stream += [("w8", W_OUT + o) for o in range(8)]
            for j in range(NJ):
                stream += [("w8", W_UP1 + j), ("w8", W_UP1 + NJ + j)]
            stream += [("wd", 8 + o) for o in range(8)]
        st = {"pos": 0, "emitted": 0, "n_w8": 0, "n_wd": 0, "slot": {}}
        LOOK = 4

        def _emit_load(i):
            kind, idx = stream[i]
            ensure_casts((kind, idx // 8 if kind == "w8" else idx // 4), 2)
            if kind == "w8":
                s = st["n_w8"] % 5
                st["n_w8"] += 1
                P.dma("sp", lambda e, s=s, idx=idx: e.dma_start(
                    out=w8s[s][:, :, :], in_=w8b[idx].rearrange("p (k c) -> p k c", k=8)),
                    reads=[("w8b", idx // 8)], writes=[("w8s", s)])
            else:
                s = st["n_wd"] % 2
                st["n_wd"] += 1
                P.dma("sp", lambda e, s=s, idx=idx: e.dma_start(
                    out=WDS[s], in_=wdb[idx].rearrange("p (j c) -> p j c", j=NJ)),
                    reads=[("wdb", idx // 4), RXL], writes=[("wds", s)])
            st["slot"][i] = s

        def wnext(kind, idx):
            i = st["pos"]
            assert stream[i] == (kind, idx), (i, stream[i], kind, idx)
            lim = min(len(stream), i + LOOK + 1)
            used = st.setdefault("used", {"w8": 0, "wd": 0})
            pool_sz = {"w8": 5, "wd": 2}
            while st["emitted"] < lim:
                k2 = stream[st["emitted"]][0]
                if k2 == "wd" and not st.get("wd_ok", False) and st["emitted"] > i:
                    break
                if st["n_" + k2] - used[k2] >= pool_sz[k2]:
                    break
                _emit_load(st["emitted"])
                st["emitted"] += 1
            if st["emitted"] <= i:
                _emit_load(i)
                st["emitted"] = i + 1
            st["pos"] += 1
            used[kind] += 1
            s = st["slot"][i]
            if kind == "w8":
                return w8s[s], ("w8s", s)
            return WDS[s], ("wds", s)

        P.dma("sp", lambda e: e.dma_start(out=vecs[:, :], in_=vecs_in[:, :]), writes=["vecs"])
        P.dma("sp", lambda e: e.dma_start(out=pcs[:, :], in_=pcm[:, :]), writes=["pcs"])
        P.dma("sp", lambda e: e.dma_start(out=csil[:, :, :], in_=cv[:, :, :]), writes=["csil"])
        P.op("dve", lambda e: e.memset(ones[:, :], 1.0), writes=["ones"])
        P.op("dve", lambda e: e.memset(epsb[:, :], float(D * EPS)), writes=["epsb"])
        csb = sb("csb", [128, 8, 2], BF16)
        ada_st = [sb("adast%d" % i, [128, 8, 128], BF16) for i in range(3)]
        P.op("act", lambda e: e.activation(out=csb[:, :, :], in_=csil[:, :, :], func=AF.Silu),
             reads=["csil"], writes=["csb"])
        ada_cnt = [0]

        def ada_chunk(l, oc, fixed_bank=None):
            ch = l * 48 + oc
            b = ada_cnt[0] % 3
            ada_cnt[0] += 1
            P.dma("pool", lambda e: e.dma_start(out=ada_st[b][:, :, :], in_=ada_in[ch].rearrange("p (k c) -> p k c", k=8)),
                  writes=[("adast", b)])
            pb = fixed_bank if fixed_bank is not None else bank()
            for kc in range(8):
                P.op("pe", lambda e, kc=kc: e.matmul(
                    pbank(pb, 2), lhsT=ada_st[b][:, kc, :], rhs=csb[:, kc, :],
                    start=(kc == 0), stop=(kc == 7)),
                    reads=[("adast", b), "csb"], writes=[("ps", pb)])
            P.op("dve", lambda e: e.tensor_scalar(
                out=MOD[:, l, oc, :], in0=pbank(pb, 2), scalar1=vcol(V_ADAB + l * 48 + oc), scalar2=None,
                op0=ALU.add), reads=[("ps", pb), "vecs"], writes=["MOD"])

        P.op("dve", lambda e: e.tensor_scalar(out=G32[:, 0:2, :], in0=vecs[:, V_N1:V_N1 + 16].rearrange("p (l k) -> p l k", l=2),
                                              scalar1=32.0, scalar2=None, op0=ALU.mult), reads=["vecs"], writes=["G32a"])
        P.op("dve", lambda e: e.tensor_scalar(out=FG32[:, :], in0=vecs[:, V_FG:V_FG + 8],
                                              scalar1=32.0, scalar2=None, op0=ALU.mult), reads=["vecs"], writes=["FG32"])
        G32b = sb("G32b", [128, 2, 8], F32)
        P.op("dve", lambda e: e.tensor_scalar(out=G32b[:, :, :], in0=vecs[:, V_N2:V_N2 + 16].rearrange("p (l k) -> p l k", l=2),
                                              scalar1=32.0, scalar2=None, op0=ALU.mult), reads=["vecs"], writes=["G32b"])

        def gm_op(l, which):
            for s in range(2):
                if which == 0:
                    P.op("dve", lambda e, s=s: e.scalar_tensor_tensor(
                        out=GM[:, l, 0, s, :], in0=MOD[:, l, 8:16, s], scalar=1.0, in1=G32[:, l, :],
                        op0=ALU.add, op1=ALU.mult), reads=["MOD", "G32a"], writes=[("GM", l, 0, s)])
                else:
                    P.op("dve", lambda e, s=s: e.scalar_tensor_tensor(
                        out=GM[:, l, 1, s, :], in0=MOD[:, l, 32:40, s], scalar=1.0, in1=G32b[:, l, :],
                        op0=ALU.add, op1=ALU.mult), reads=["MOD", "G32b"], writes=[("GM", l, 1, s)])

        ada_rest = []
        for l in range(2):
            for oc in range(48):
                if l == 0 and oc < 16:
                    continue
                ada_rest.append(lambda fb, l=l, oc=oc: ada_chunk(l, oc, fb))
                if oc == 15:
                    ada_rest.append(lambda fb, l=l: gm_op(l, 0))
                if oc == 39:
                    ada_rest.append(lambda fb, l=l: gm_op(l, 1))
        for oc in range(16):
            ada_chunk(0, oc)
        gm_op(0, 0)
        cast_order = ([("w8", g) for g in (0, 1, 2, 3)] + [("w8", g) for g in range(8, 14)] + [("wd", 0), ("wd", 1)]
                      + [("w8", g) for g in (4, 5, 6, 7)] + [("w8", g) for g in range(14, 19)] + [("wd", 2), ("wd", 3)])
        cast_pos = {k: i for i, k in enumerate(cast_order)}
        cast_state = {"n": 0}

        def ensure_casts(key, ahead=2):
            lim = min(len(cast_order), cast_pos[key] + 1 + ahead)
            while cast_state["n"] < lim:
                kind, g = cast_order[cast_state["n"]]
                cast_state["n"] += 1
                if kind == "w8":
                    P.dma("pool", lambda e, g=g: e.dma_start(
                        out=w8b[g * 8:(g + 1) * 8].rearrange("n p c -> (n p) c"),
                        in_=w8_in[g * 8:(g + 1) * 8].rearrange("n p c -> (n p) c")),
                        reads=[], writes=[("w8b", g)])
                else:
                    P.dma("pool", lambda e, g=g: e.dma_start(
                        out=wdb[g * 4:(g + 1) * 4].rearrange("n p c -> (n p) c"),
                        in_=wd_in[g * 4:(g + 1) * 4].rearrange("n p c -> (n p) c")),
                        reads=[], writes=[("wdb", g)])

        ensure_casts(("w8", 0), 2)
        P.dma("pool", lambda e: e.dma_start(out=ident[:, :], in_=ident_in[:, :]), writes=["ident"])
        P.dma("pool", lambda e: e.dma_start(out=mtb[:, :, :], in_=mt_in[:, :, :]), writes=["mtb"])
        tab_done = [False] * 8
        MODR = ["MOD"] + [("GM", l, w, s) for l in range(2) for w in range(2) for s in range(2)]
        dump("MOD", MOD[:, :, :, :], MODR)
        dump("GM", GM[:, :, :, :, :].rearrange("p a b c d -> p (a b c d)"), MODR)

        def modcol(l, oc, s):
            return MOD[:, l, oc, s:s + 1]

        def hgran(c0, c1):
            return [("h", i) for i in range(c0 // 128, (c1 + 127) // 128)]

        nb_rr = [0]
        EXT_BLK = [(0, 344), (344, 688), (688, 1032), (1032, 1376), (1376, 1664)]

        def _affine(i, n, gm_fn, sh_fn, dst_fn, dst_res):
            for k in range(8):
                gm = gm_fn(k)
                sh = sh_fn(k)
                if sh is None:
                    P.op("act", lambda e, k=k, gm=gm: e.activation(out=dst_fn(k), in_=xblk[i][:, k, 0:n], func=AF.Identity, scale=gm),
                         reads=[("xb", i)] + MODR + ["FG32"], writes=dst_res)
                else:
                    P.op("act", lambda e, k=k, gm=gm, sh=sh: e.activation(out=dst_fn(k), in_=xblk[i][:, k, 0:n], func=AF.Identity,
                                                                       scale=gm, bias=sh),
                         reads=[("xb", i)] + MODR, writes=dst_res)

        def _ss_and_rstd_begin(c0, c1, src, sres):
            n = c1 - c0
            P.op("act", lambda e: e.activation(out=h_ext[:, :, c0:c1], in_=src, func=AF.Square),
                 reads=sres, writes=hgran(c0, c1))
            pb = bank()
            for kc in range(8):
                P.op("pe", lambda e, kc=kc: e.matmul(pbank(pb, n), lhsT=ones[:, :], rhs=h_ext[:, kc, c0:c1],
                                                     start=(kc == 0), stop=(kc == 7)),
                     reads=hgran(c0, c1) + ["ones"], writes=[("ps", pb)])
            return pb

        def _rstd_finish(c0, c1, pb):
            n = c1 - c0
            rr = ("rstd", c0 // 344)
            P.op("act", lambda e: e.activation(out=rstd[:, c0:c1], in_=pbank(pb, n), func=AF.Ln, bias=epsb[:, 0:1], scale=1.0),
                 reads=[("ps", pb), "epsb"], writes=[rr])
            P.op("act", lambda e: e.activation(out=rstd[:, c0:c1], in_=rstd[:, c0:c1], func=AF.Exp, scale=-0.5),
                 reads=[rr], writes=[rr])

        def norm_ext(xin, pt0, s):
            pbs = []
            for (c0, c1) in EXT_BLK:
                n = c1 - c0
                i = nb_rr[0] % 2
                nb_rr[0] += 1
                P.dma("sp", lambda e, i=i, c0=c0, c1=c1, n=n: e.dma_start(out=xblk[i][:, :, 0:n], in_=xin[:, :, pt0 + c0:pt0 + c1]),
                      writes=[("xb", i)])
                pbs.append(_ss_and_rstd_begin(c0, c1, xblk[i][:, :, 0:n], [("xb", i)]))
            for b, (c0, c1) in enumerate(EXT_BLK):
                _rstd_finish(c0, c1, pbs[b])
            for b, (c0, c1) in enumerate(EXT_BLK):
                n = c1 - c0
                i = nb_rr[0] % 2
                nb_rr[0] += 1
                P.dma("sp", lambda e, i=i, c0=c0, c1=c1, n=n: e.dma_start(out=xblk[i][:, :, 0:n], in_=xin[:, :, pt0 + c0:pt0 + c1]),
                      writes=[("xb", i)])
                P.op("dve", lambda e, i=i, c0=c0, c1=c1, n=n: e.tensor_tensor(
                    out=xblk[i][:, :, 0:n], in0=xblk[i][:, :, 0:n],
                    in1=rstd[:, c0:c1].unsqueeze(1).to_broadcast([128, 8, n]), op=ALU.mult),
                    reads=[("xb", i), ("rstd", b)], writes=[("xb", i)])
                _affine(i, n, lambda k: GM[:, 0, 0, s, k:k + 1], lambda k: modcol(0, k, s),
                        lambda k, c0=c0, c1=c1: h_ext[:, k, c0:c1], hgran(c0, c1))

        SHC = NRES

        def norm_res(blocks, gm_fn, sh_ap, kind, yout=None, pt0=0, tag=None):
            while ss_pend:
                ss_pend.pop(0)()
            for b in (2, 0, 1):
                f0, f1 = RES_BLK[b]
                _rstd_finish(f0, f1, SSB[b])
            if sh_ap is not None:
                P.op("dve", lambda e: e.tensor_copy(out=h_ext[:, :, SHC:SHC + 1], in_=sh_ap),
                     reads=MODR, writes=[("h", SHC // 128)])
            for b in (2, 0, 1):
                c0, c1 = blocks[b]
                n = c1 - c0
                if kind == "h":
                    for k in range(8):
                        P.op("dve", lambda e, k=k, c0=c0, c1=c1: e.scalar_tensor_tensor(
                            out=h_ext[:, k, c0:c1], in0=x_res[:, k, c0:c1], scalar=gm_fn(k), in1=rstd[:, c0:c1],
                            op0=ALU.mult, op1=ALU.mult),
                            reads=xres_of(c0, c1) + [("rstd", c0 // 344)] + MODR, writes=hgran(c0, c1))
                else:
                    i = nb_rr[0] % 2
                    nb_rr[0] += 1
                    for k in range(8):
                        P.op("dve", lambda e, k=k, c0=c0, c1=c1, i=i, n=n: e.scalar_tensor_tensor(
                            out=xblk[i][:, k, 0:n], in0=x_res[:, k, c0:c1], scalar=gm_fn(k), in1=rstd[:, c0:c1],
                            op0=ALU.mult, op1=ALU.mult),
                            reads=xres_of(c0, c1) + [("rstd", c0 // 344), "FG32"], writes=[("xb", i)])
                    P.dma("sp", lambda e, i=i, n=n, c0=c0: e.dma_start(
                        out=yout[:, :, pt0 + c0 - 3:pt0 + c0 - 3 + n], in_=xblk[i][:, :, 0:n]),
                        reads=[("xb", i)], writes=[("y", tag, c0)])

        bc_rr = [0]

        def dense_fm_bias(wt, wres, evac):
            bi = bc_rr[0] % 4
            bc_rr[0] += 1
            bcol = bcolb[bi]
            order = [RES_BLK[2], RES_BLK[0], RES_BLK[1]]
            for oi, (c0, c1) in enumerate(order):
                pb = bank()
                ce = c1 + 1 if oi == 0 else c1
                n = ce - c0
                for kc in range(8):
                    P.op("pe", lambda e, kc=kc, pb=pb, c0=c0, ce=ce, n=n: e.matmul(
                        pbank(pb, n), lhsT=wt[:, kc, :], rhs=h_ext[:, kc, c0:ce], start=(kc == 0), stop=(kc == 7)),
                        reads=[wres] + hgran(c0, ce), writes=[("ps", pb)])
                if oi == 0:
                    P.op("act", lambda e, pb=pb, n=n: e.activation(out=bcol[:, :], in_=ps[:, pb * 512 + n - 1:pb * 512 + n], func=AF.Copy),
                         reads=[("ps", pb)], writes=[("bcol", bi)])
                evac(pb, c0, c1, bcol, ("bcol", bi))

        SSB = (4, 5, 6)
        ss_pend = []
        DBANKS = (0, 1, 2, 3)

        def resid_evac(gate_ap, oc, pb, c0, c1):
            n = c1 - c0
            P.op("dve", lambda e: e.scalar_tensor_tensor(
                out=x_res[:, oc, c0:c1], in0=pbank(pb, n), scalar=gate_ap, in1=x_res[:, oc, c0:c1],
                op0=ALU.mult, op1=ALU.add), reads=[("ps", pb)] + MODR + xres_of(c0, c1), writes=xres_of(c0, c1))
            b = [i for i, (a0, a1) in enumerate(RES_BLK) if a0 <= c0 < a1][0]
            f0, f1 = RES_BLK[b]
            nf = f1 - f0
            P.op("act", lambda e: e.activation(out=h_ext[:, oc, f0:f1], in_=x_res[:, oc, f0:f1], func=AF.Square),
                 reads=xres_of(f0, f1), writes=hgran(f0, f1))
            ss_pend.append(lambda: P.op("pe", lambda e: e.matmul(pbank(SSB[b], nf), lhsT=ones[:, :], rhs=h_ext[:, oc, f0:f1],
                                                                 start=(oc == 0), stop=(oc == 7)),
                                        reads=hgran(f0, f1) + ["ones"], writes=[("ps", SSB[b])]))

        def dense_fm(wt, wres, rhs_fn, rhs_res_fn, blocks, nk, evac, banks=(0, 1, 2, 3, 4, 5, 6)):
            for (c0, c1) in blocks:
                while len(ss_pend) > 8:
                    ss_pend.pop(0)()
                pb = bank(banks)
                n = c1 - c0
                for kc in range(nk):
                    P.op("pe", lambda e, kc=kc, pb=pb, c0=c0, c1=c1, n=n: e.matmul(
                        pbank(pb, n), lhsT=wt[:, kc, :], rhs=rhs_fn(kc, c0, c1), start=(kc == 0), stop=(kc == nk - 1)),
                        reads=[wres] + rhs_res_fn(c0, c1), writes=[("ps", pb)])
                evac(pb, c0, c1)

        def xres_of(c0, c1):
            return [("x", b) for b, (a0, a1) in enumerate(RES_BLK) if a0 < c1 and c0 < a1]

        def tile_prog(seg, t):
            s = seg
            xin = xp if seg == 0 else xs
            yout = yp if seg == 0 else ys
            rows = P_ROWS if seg == 0 else S_ROWS
            pt0 = TT * t
            first = (t == 0)
            last = (t == (3 if seg == 0 else 1))
            if seg == 0:
                plans = [_pair_plan_static(TR * t, P_ROWS, j) for j in range(NPAIR)]
            else:
                plans = splans[t]

            fence()
            st["wd_ok"] = False
            for b, (c0, c1) in enumerate(RES_BLK):
                P.dma("sp", lambda e, c0=c0, c1=c1: e.dma_start(out=x_res[:, :, c0:c1],
                                                              in_=xin[:, :, pt0 + RES0 + c0:pt0 + RES0 + c1]),
                      writes=[("x", b)])
            P.dma("sp", lambda e: e.dma_start(out=BVB, in_=bvb_in[:, :]), reads=[RXL], writes=["bvb"])
            if seg == 1:
                P.dma("sp", lambda e: e.dma_start(out=MTS, in_=mtb[:, :, :]), reads=["mtb", RXL], writes=["mts"])
            norm_ext(xin, pt0, s)
            dump("h_ext_%d_%d" % (seg, t), h_ext[:, :, :], hgran(0, EXT))
            for vb in range(2):
                P.op("dve", lambda e, vb=vb: e.memset(VA[vb][:, :, :, 64:65], 1.0), reads=[RXL], writes=[("va", vb)])
            QBLK = [(QOFF, QOFF + 384), (QOFF + 384, QOFF + 768), (QOFF + 768, QOFF + 1152)]
            KBLK = [(0, 416), (416, 832), (832, 1248), (1248, 1664)]

            def qkv_groups(c, fixed_bank):
                qb = c % 2
                hold = {}
                gs = []

                def pick():
                    return fixed_bank if fixed_bank is not None else bank()

                for bi, (c0, c1) in enumerate(QBLK):
                    def g(bi=bi, c0=c0, c1=c1):
                        if bi == 0:
                            if tab_done[c]:
                                P.dma("sp", lambda e: e.dma_start(out=TAB[qb], in_=tabb[:, 2 * c:2 * c + 2, :]),
                                      reads=[("tabb", c), RXL], writes=[("tab", qb)])
                            else:
                                tab_done[c] = True
                                P.dma("sp", lambda e: e.dma_start(out=STG, in_=btab_in[:, 2 * c:2 * c + 2, :]),
                                      reads=[RXL], writes=["stg"])
                                P.op("act", lambda e: e.activation(out=TAB[qb], in_=STG, func=AF.Exp),
                                     reads=["stg", RXL], writes=[("tab", qb)])
                                P.dma("sp", lambda e: e.dma_start(out=tabb[:, 2 * c:2 * c + 2, :], in_=TAB[qb]),
                                      reads=[("tab", qb), RXL], writes=[("tabb", c)])
                            hold["q"] = wnext("w8", W_Q + c)
                        wt, wres = hold["q"]
                        pb = pick()
                        n = c1 - c0
                        for kc in range(8):
                            P.op("pe", lambda e, kc=kc: e.matmul(
                                pbank(pb, n), lhsT=wt[:, kc, :], rhs=h_ext[:, kc, c0:c1], start=(kc == 0), stop=(kc == 7)),
                                reads=[wres] + hgran(c0, c1), writes=[("ps", pb)])
                        P.op("act", lambda e: e.activation(
                            out=QT[qb][:, c0 - QOFF:c1 - QOFF], in_=pbank(pb, n), func=AF.Identity, bias=vcol(V_BQK + c), scale=1.0),
                            reads=[("ps", pb), "vecs", RXL], writes=[("qt", qb)])
                    gs.append(g)
                for bi, (c0, c1) in enumerate(KBLK):
                    def g(bi=bi, c0=c0, c1=c1):
                        if bi == 0:
                            hold["k"] = wnext("w8", W_K + c)
                        wt, wres = hold["k"]
                        pb = pick()
                        n = c1 - c0
                        for kc in range(8):
                            P.op("pe", lambda e, kc=kc: e.matmul(
                                pbank(pb, n), lhsT=wt[:, kc, :], rhs=h_ext[:, kc, c0:c1], start=(kc == 0), stop=(kc == 7)),
                                reads=[wres] + hgran(c0, c1), writes=[("ps", pb)])
                        P.op("act", lambda e: e.activation(
                            out=KT[qb][:, c0:c1], in_=pbank(pb, n), func=AF.Identity, bias=vcol(V_BQK + 8 + c), scale=1.0),
                            reads=[("ps", pb), "vecs", RXL], writes=[("kt", qb)])
                    gs.append(g)
                for m0 in range(0, NCH, 4):
                    def g(m0=m0):
                        if m0 == 0:
                            hold["v"] = wnext("w8", W_V + c)
                        wt, wres = hold["v"]
                        mm = min(4, NCH - m0)
                        pb = pick()
                        for mi in range(mm):
                            m = m0 + mi
                            for kc in range(8):
                                P.op("pe", lambda e, kc=kc, m=m, mi=mi: e.matmul(
                                    ps[:, pb * 512 + mi * 128:pb * 512 + (mi + 1) * 128], lhsT=h_ext[:, kc, m * 128:(m + 1) * 128],
                                    rhs=wt[:, kc, :], start=(kc == 0), stop=(kc == 7)),
                                    reads=[wres, ("h", m)], writes=[("ps", pb)])
                        P.op("dve", lambda e: e.tensor_tensor(
                            out=VA[qb][:, m0:m0 + mm, :, 0:64],
                            in0=ps[:, pb * 512:pb * 512 + mm * 128].rearrange("p (m h d) -> p m h d", h=2, d=64),
                            in1=BVB[:, c * 128:(c + 1) * 128].rearrange("p (h d) -> p h d", h=2).unsqueeze(1).to_broadcast([128, mm, 2, 64]),
                            op=ALU.add), reads=[("ps", pb), "bvb", RXL], writes=[("va", qb)])
                    gs.append(g)
                return gs

            def emit_qk_sm(c, j, hh, ui):
                qb = c % 2
                pl = plans[j]
                n = pl["n"]
                sbk = (ui % 2) * 2
                eb = ui % 2
                hp0, hp1 = hh * 64, hh * 64 + 64
                for i in range(n):
                    cc = pl["c_hi"] - i
                    P.op("pe", lambda e, i=i, cc=cc: e.matmul(
                        ps[:, sbk * 512 + i * 128:sbk * 512 + (i + 1) * 128],
                        lhsT=KT[qb][hp0:hp1, cc * 128:(cc + 1) * 128],
                        rhs=QT[qb][hp0:hp1, j * 128:(j + 1) * 128], start=True, stop=True),
                        reads=[("kt", qb), ("qt", qb)], writes=[("ps", sbk), ("ps", sbk + 1)])
                P.op("act", lambda e: e.activation(
                    out=EB[eb][:, 0:n * 128], in_=ps[:, sbk * 512:sbk * 512 + n * 128], func=AF.Exp, scale=0.125),
                    reads=[("ps", sbk), ("ps", sbk + 1), RXL], writes=[("eb", eb)])
                s0 = pl["s0"]
                P.op("dve", lambda e: e.tensor_tensor(
                    out=PB[eb][:, 0:n * 128], in0=EB[eb][:, 0:n * 128],
                    in1=TAB[qb][:, hh, s0 * 64:s0 * 64 + n * 128], op=ALU.mult),
                    reads=[("eb", eb), ("tab", qb)], writes=[("pb", eb)])
                if pl["dyn"] is not None:
                    dy = pl["dyn"]
                    P.op("dve", lambda e: e.tensor_tensor(
                        out=PB[eb][:, 0:n * 128], in0=PB[eb][:, 0:n * 128], in1=MTS[:, dy, 0:n * 128], op=ALU.mult),
                        reads=[("pb", eb), "mts"], writes=[("pb", eb)])

            def emit_pv(c, j, hh, ui):
                qb = c % 2
                pl = plans[j]
                n = pl["n"]
                eb = ui % 2
                ob = 4 + (j % 2)
                if pl["dyn"] is not None:
                    for i in range(n):
                        cc = pl["c_hi"] - i
                        P.op("pe", lambda e, i=i, cc=cc: e.matmul(
                            ps[:, ob * 512 + hh * 65:ob * 512 + hh * 65 + 65],
                            lhsT=PB[eb][:, i * 128:(i + 1) * 128], rhs=VA[qb][:, cc, hh, :],
                            start=(i == 0), stop=(i == n - 1)),
                            reads=[("pb", eb), ("va", qb)], writes=[("ps", ob)])
                else:
                    for qi in range(2):
                        lst = pl["pv"][qi]
                        for li, (i, mode) in enumerate(lst):
                            cc = pl["c_hi"] - i
                            k0, k1 = {"both": (0, 128), "top": (0, 64), "bot": (64, 128)}[mode]
                            P.op("pe", lambda e, i=i, cc=cc, qi=qi, k0=k0, k1=k1, li=li, nl=len(lst): e.matmul(
                                ps[qi * 64:(qi + 1) * 64, ob * 512 + hh * 65:ob * 512 + hh * 65 + 65],
                                lhsT=PB[eb][k0:k1, i * 128 + qi * 64:i * 128 + qi * 64 + 64],
                                rhs=VA[qb][k0:k1, cc, hh, :], start=(li == 0), stop=(li == nl - 1)),
                                reads=[("pb", eb), ("va", qb)], writes=[("ps", ob)])
                if hh == 1:
                    rb = j % 2
                    ov = ps[:, ob * 512:ob * 512 + 130].rearrange("p (h d) -> p h d", d=65)
                    P.op("dve", lambda e: e.reciprocal(out=rec[rb][:, :], in_=ov[:, :, 64]),
                         reads=[("ps", ob)], writes=[("rec", rb)])
                    P.op("dve", lambda e: e.tensor_tensor(
                        out=OSB[:, j, c * 128:(c + 1) * 128].rearrange("p (h d) -> p h d", d=64),
                        in0=ov[:, :, 0:64], in1=rec[rb][:, :].unsqueeze(2).to_broadcast([128, 2, 64]), op=ALU.mult),
                        reads=[("ps", ob), ("rec", rb), RXL], writes=[("osb", j)])

            for g in qkv_groups(0, None):
                g()
            units = [(j, hh) for j in range(NPAIR) for hh in range(2)]
            for c in range(8):
                nxt = qkv_groups(c + 1, 6) if c < 7 else []
                for ui, (j, hh) in enumerate(units):
                    emit_qk_sm(c, j, hh, ui)
                    if ui >= 1:
                        emit_pv(c, units[ui - 1][0], units[ui - 1][1], ui - 1)
                    if nxt:
                        nxt.pop(0)()
                    if ada_rest:
                        ada_rest.pop(0)(6)
                emit_pv(c, units[-1][0], units[-1][1], len(units) - 1)
                while nxt:
                    nxt.pop(0)()
            while ada_rest:
                ada_rest.pop(0)(6)
            for j in range(NPAIR):
                for oc in range(8):
                    P.op("pe", lambda e, j=j, oc=oc: e.transpose(out=psT[:, oc * 128:(oc + 1) * 128],
                                                                 in_=OSB[:, j, oc * 128:(oc + 1) * 128], identity=ident[:, :]),
                         reads=[("osb", j), "ident"], writes=["psT"])
                P.op("act", lambda e, j=j: e.activation(out=OT[:, :, j * 128:(j + 1) * 128],
                                                        in_=psT[:, :].rearrange("p (k q) -> p k q", k=8), func=AF.Copy),
                     reads=["psT", RXL], writes=[("ot", j)])

            dump("OT_%d_%d" % (seg, t), OT, [("ot", jj) for jj in range(NPAIR)])
            dump("OSB_%d_%d" % (seg, t), OSB, [("osb", jj) for jj in range(NPAIR)])

            def otres(c0, c1):
                a, b = c0 + RES0 - QOFF, c1 + RES0 - QOFF
                return [("ot", jj) for jj in range(a // 128, (b + 127) // 128)]

            for oc in range(8):
                wt, wres = wnext("w8", W_O + oc)

                def evac(pb, c0, c1, oc=oc):
                    resid_evac(modcol(0, 16 + oc, s), oc, pb, c0, c1)
                dense_fm(wt, wres, lambda kc, c0, c1: OT[:, kc, c0 + RES0 - QOFF:c1 + RES0 - QOFF],
                         lambda c0, c1: otres(c0, c1) + [RXL], RES_BLK, 8, evac, banks=DBANKS)

            dump("x0m_%d_%d" % (seg, t), x_res[:, :, :], xres_of(0, NRES))
            fence()
            st["wd_ok"] = True
            ffn(0, s, seg, first, last)
            dump("x1_%d_%d" % (seg, t), x_res[:, :, :], xres_of(0, NRES))
            sc_mixer(s, seg, first, last)
            dump("x1m_%d_%d" % (seg, t), x_res[:, :, :], xres_of(0, NRES))
            ffn(1, s, seg, first, last)
            dump("x2_%d_%d" % (seg, t), x_res[:, :, :], xres_of(0, NRES))
            norm_res([(3, 344), (344, 688), (688, 1027)], lambda k: FG32[:, k:k + 1], None, "y",
                     yout=yout, pt0=pt0, tag=(seg, t))

        def pre_norm(l, which, s):
            base = 0 if which == 0 else 24
            norm_res(RES_BLK, lambda k: GM[:, l, which, s, k:k + 1], MOD[:, l, base:base + 8, s:s + 1], "h")

        def edge_fix(buf, res, seg, first, last):
            if seg == 0:
                if first:
                    P.op("dve", lambda e: e.memset(buf[:, 2:3], 0.0), reads=[], writes=[res])
                if last:
                    P.op("dve", lambda e: e.memset(buf[:, 1027:1028], 0.0), reads=[], writes=[res])
            else:
                if first:
                    P.op("dve", lambda e: e.tensor_scalar(out=buf[:, 2:3], in0=buf[:, 2:3], scalar1=pcs[:, 0:1], scalar2=None,
                                                          op0=ALU.mult), reads=[res, "pcs"], writes=[res])
                if last:
                    P.op("dve", lambda e: e.tensor_scalar(out=buf[:, 1027:1028], in0=buf[:, 1027:1028], scalar1=pcs[:, 1:2],
                                                          scalar2=None, op0=ALU.mult), reads=[res, "pcs"], writes=[res])

        def conv3(src, dst, w0, w1, w2, rs, rd):
            P.op("dve", lambda e: e.tensor_scalar(out=dst[:, 1:NRES - 1], in0=src[:, 1:NRES - 1], scalar1=w1, scalar2=None,
                                                  op0=ALU.mult), reads=[rs, "vecs"], writes=[rd])
            P.op("dve", lambda e: e.scalar_tensor_tensor(out=dst[:, 1:NRES - 1], in0=src[:, 0:NRES - 2], scalar=w0,
                                                         in1=dst[:, 1:NRES - 1], op0=ALU.mult, op1=ALU.add),
                 reads=[rs, rd, "vecs"], writes=[rd])
            P.op("dve", lambda e: e.scalar_tensor_tensor(out=dst[:, 1:NRES - 1], in0=src[:, 2:NRES], scalar=w2,
                                                         in1=dst[:, 1:NRES - 1], op0=ALU.mult, op1=ALU.add),
                 reads=[rs, rd, "vecs"], writes=[rd])

        def ffn(l, s, seg, first, last):
            pre_norm(l, 1, s)
            if l == 0:
                dump("h2", h_ext[:, :, 0:NRES], hgran(0, NRES))
            wup = W_UP0 if l == 0 else W_UP1
            for j in range(NJ):
                par = j % 2
                A, Vv, Tt = TMP[par]
                rA, rV, rT = ("tA", par), ("tV", par), ("tT", par)
                for (widx, dst, rdst) in ((wup + j, A, rA), (wup + NJ + j, Vv, rV)):
                    wt, wres = wnext("w8", widx)

                    def evac(pb, c0, c1, bcol, bres, dst=dst, rdst=rdst):
                        n = c1 - c0
                        P.op("act", lambda e: e.activation(out=dst[:, c0:c1], in_=pbank(pb, n), func=AF.Identity,
                                                           bias=bcol[:, 0:1], scale=1.0),
                             reads=[("ps", pb), bres, RXL], writes=[rdst])
                    dense_fm_bias(wt, wres, evac)
                edge_fix(A, rA, seg, first, last)
                if l == 0 and j == 0:
                    dump("ffn_A", A[:, 0:NRES], [rA])
                    dump("ffn_V", Vv[:, 0:NRES], [rV])
                cw = V_FCW + l * 66
                conv3(A, Tt, vcol(cw + j), vcol(cw + 22 + j), vcol(cw + 44 + j), rA, rT)
                P.op("act", lambda e, Tt=Tt, l=l, j=j: e.activation(out=Tt[:, 1:NRES - 1], in_=Tt[:, 1:NRES - 1], func=AF.Silu,
                                                                   bias=vcol(V_FCB + l * 22 + j), scale=1.0),
                     reads=[rT, "vecs", RXL], writes=[rT])
                P.op("dve", lambda e, Tt=Tt, Vv=Vv, j=j: e.tensor_tensor(out=GB[:, j, 1:NRES - 1], in0=Tt[:, 1:NRES - 1],
                                                                         in1=Vv[:, 1:NRES - 1], op=ALU.mult),
                     reads=[rT, rV, RXL], writes=[("g", j)])
                if l == 0 and j == 0:
                    dump("ffn_T", Tt[:, 0:NRES], [rT])
                    dump("ffn_G", GB[:, 0, :], [("g", 0)])
            DBLK = [(1, 344), (344, 688), (688, 1029)]
            for oc in range(8):
                wt, wres = wnext("wd", l * 8 + oc)

                def evac(pb, c0, c1, oc=oc):
                    resid_evac(modcol(l, 40 + oc, s), oc, pb, c0, c1)
                dense_fm(wt, wres, lambda kc, c0, c1: GB[:, kc, c0:c1], lambda c0, c1: [("g", jj) for jj in range(NJ)] + [RXL],
                         DBLK, NJ, evac, banks=DBANKS)

        def sc_mixer(s, seg, first, last):
            pre_norm(1, 0, s)
            for k in range(8):
                par = k % 2
                A, Vv, Tt = TMP[par]
                rA, rV, rT = ("tA", par), ("tV", par), ("tT", par)
                for (widx, dst, rdst) in ((W_IN + 8 + k, A, rA), (W_IN + 16 + k, Vv, rV)):
                    wt, wres = wnext("w8", widx)

                    def evac(pb, c0, c1, bcol, bres, dst=dst, rdst=rdst):
                        n = c1 - c0
                        P.op("act", lambda e: e.activation(out=dst[:, c0:c1], in_=pbank(pb, n), func=AF.Identity,
                                                           bias=bcol[:, 0:1], scale=1.0),
                             reads=[("ps", pb), bres, RXL], writes=[rdst])
                    dense_fm_bias(wt, wres, evac)
                P.op("dve", lambda e, A=A, Vv=Vv: e.tensor_tensor(out=A[:, 0:NRES], in0=A[:, 0:NRES], in1=Vv[:, 0:NRES], op=ALU.mult),
                     reads=[rA, rV], writes=[rA])
                edge_fix(A, rA, seg, first, last)
                conv3(A, Tt, vcol(V_SCW + k), vcol(V_SCW + 8 + k), vcol(V_SCW + 16 + k), rA, rT)
                def evac_b(pb, c0, c1, bcol, bres, Vv=Vv, rV=rV):
                    n = c1 - c0
                    P.op("act", lambda e: e.activation(out=Vv[:, c0:c1], in_=pbank(pb, n), func=AF.Identity,
                                                       bias=bcol[:, 0:1], scale=1.0),
                         reads=[("ps", pb), bres, RXL], writes=[rV])
                wtb, wrb = wnext("w8", W_IN + k)
                dense_fm_bias(wtb, wrb, evac_b)
                P.op("dve", lambda e, Tt=Tt, Vv=Vv, k=k: e.tensor_tensor(out=ZB[:, k, 1:NRES - 1], in0=Tt[:, 1:NRES - 1],
                                                                         in1=Vv[:, 1:NRES - 1], op=ALU.mult),
                     reads=[rT, rV, RXL], writes=[("z", k)])
            DBLK = [(1, 344), (344, 688), (688, 1029)]
            for oc in range(8):
                wt, wres = wnext("w8", W_OUT + oc)

                def evac(pb, c0, c1, oc=oc):
                    resid_evac(modcol(1, 16 + oc, s), oc, pb, c0, c1)
                dense_fm(wt, wres, lambda kc, c0, c1: ZB[:, kc, c0:c1], lambda c0, c1: [("z", kk) for kk in range(8)] + [RXL],
                         DBLK, 8, evac, banks=DBANKS)

        for (seg, t) in tiles:
            tile_prog(seg, t)
        P.emit()
    return nc, P


def _fm(x):
    T = x.shape[0]
    return np.ascontiguousarray(x.reshape(T, 8, 128).transpose(2, 1, 0))


def _wchunks(W):
    K, N = W.shape
    a = W.reshape(K // 128, 128, N // 128, 128).transpose(2, 1, 0, 3)
    return np.ascontiguousarray(a).reshape(N // 128, 128, (K // 128) * 128)


def _pvec(v):
    return np.ascontiguousarray(v.reshape(-1, 128).T)


def _bias_table(rpb):
    kc = np.arange(64)[:, None]
    qc = np.arange(64)[None, :]
    cs = np.clip(qc - 8, 0, 48)
    cvalid = (kc >= cs) & (kc < cs + 16)
    dc = np.clip(kc - qc + 15, 0, 30)
    out = np.full((128, NH, NSLOT, 64), NEG, np.float32)
    for s in range(NSLOT):
        for half, delta in ((0, 8 - s), (1, 9 - s)):
            if -7 <= delta <= 7:
                g = rpb[:, delta + 7][:, dc]
                g = np.where(cvalid[None], g, NEG).astype(np.float32)
                out[half * 64:(half + 1) * 64, :, s, :] = g.transpose(1, 0, 2)
    return out.reshape(128, NH, NSLOT * 64)


_CACHE = {}


def kernel(x_prompt, x_sample, c_prompt, c_sample, ada_w, ada_b, norm1_g, norm2_g,
           na_w_qkv, na_b_qkv, na_rpb, na_w_o, sc_w_in, sc_conv_w, sc_w_out,
           ffn_w_up, ffn_conv_w, ffn_conv_b, ffn_w_down, final_g, _tiles=None, _debug=False):
    f = lambda a: np.asarray(a, dtype=np.float32)
    x_prompt, x_sample, c_prompt, c_sample = f(x_prompt), f(x_sample), f(c_prompt), f(c_sample)
    key = (None if _tiles is None else tuple(_tiles), _debug)
    if key not in _CACHE:
        _CACHE[key] = build_program(_tiles, _debug)
    nc, _ = _CACHE[key]
    _, masks = _sample_plans()

    w8 = np.concatenate([
        _wchunks(f(na_w_qkv)[0]),
        _wchunks(f(na_w_o)[0]),
        _wchunks(f(sc_w_in)[0]),
        _wchunks(f(sc_w_out)[0]),
        _wchunks(f(ffn_w_up)[0]),
        _wchunks(f(ffn_w_up)[1]),
    ], 0)
    assert w8.shape[0] == NW8
    wd = np.stack([np.ascontiguousarray(
        f(ffn_w_down)[l].reshape(NJ, 128, 8, 128).transpose(2, 1, 0, 3)).reshape(8, 128, NJ * 128)
        for l in range(2)], 0).reshape(16, 128, NJ * 128)
    ada = np.concatenate([_wchunks(f(ada_w)[l]) for l in range(2)], 0)
    vecs = np.zeros((128, NV), np.float32)
    for l in range(2):
        vecs[:, V_ADAB + l * 48:V_ADAB + (l + 1) * 48] = _pvec(f(ada_b)[l])
        vecs[:, V_N1 + l * 8:V_N1 + (l + 1) * 8] = _pvec(f(norm1_g)[l])
        vecs[:, V_N2 + l * 8:V_N2 + (l + 1) * 8] = _pvec(f(norm2_g)[l])
        for tap in range(3):
            vecs[:, V_FCW + l * 66 + tap * 22:V_FCW + l * 66 + (tap + 1) * 22] = _pvec(f(ffn_conv_w)[l, tap])
        vecs[:, V_FCB + l * 22:V_FCB + (l + 1) * 22] = _pvec(f(ffn_conv_b)[l])
    vecs[:, V_FG:V_FG + 8] = _pvec(f(final_g))
    vecs[:, V_BQK:V_BQK + 16] = _pvec(f(na_b_qkv)[0, :2048])
    for tap in range(3):
        vecs[:, V_SCW + tap * 8:V_SCW + (tap + 1) * 8] = _pvec(f(sc_conv_w)[0, tap])
    bvb = np.ascontiguousarray(np.broadcast_to(f(na_b_qkv)[0, 2048:][None, :], (128, D)))
    ident = np.eye(128, dtype=np.float32)
    btab = _bias_table(f(na_rpb)[0])

    xs_pad = np.zeros((S_ROWS * GW + 2 * HR * GW, D), np.float32)
    xs_pad[HR * GW:HR * GW + S_ROWS * GW] = x_sample[0]
    in_maps = []
    for c in range(8):
        xpp = np.zeros((P_ROWS * GW + 2 * HR * GW, D), np.float32)
        xpp[HR * GW:HR * GW + P_ROWS * GW] = x_prompt[c]
        xsc = xs_pad[c * S_LOC * GW:c * S_LOC * GW + S_LOC * GW + 2 * HR * GW]
        cvv = np.stack([_pvec(c_prompt[c]), _pvec(c_sample[0])], -1)
        pcm = np.zeros((128, 4), np.float32)
        pcm[:, 0] = 0.0 if c == 0 else 1.0
        pcm[:, 1] = 0.0 if c == 7 else 1.0
        mt = np.stack([m[c] for m in masks], 1)
        in_maps.append({
            "xp": _fm(xpp), "xs": _fm(xsc), "cv": np.ascontiguousarray(cvv), "pcm": pcm, "mt": np.ascontiguousarray(mt),
            "vecs": vecs, "bvb": bvb, "ident": ident, "btab": btab, "ada": ada, "w8": w8, "wd": wd,
        })
    res = run_bass_kernel_spmd(nc, in_maps, core_ids=list(range(8)))
    y_p = np.empty((8, P_ROWS * GW, D), np.float32)
    y_s = np.empty((1, S_ROWS * GW, D), np.float32)
    for c in range(8):
        r = res.results[c]
        y_p[c] = np.asarray(r["yp"]).transpose(2, 1, 0).reshape(P_ROWS * GW, D)
        y_s[0, c * S_LOC * GW:(c + 1) * S_LOC * GW] = np.asarray(r["ys"]).transpose(2, 1, 0).reshape(S_LOC * GW, D)
    if _debug:
        return (y_p, y_s), res.results
    return (y_p, y_s)
```

```python
import numpy as np
import ml_dtypes
from contextlib import ExitStack
import concourse.bass as bass
import concourse.mybir as mybir
from concourse.bass_utils import run_bass_kernel_spmd

F32 = mybir.dt.float32
BF16 = mybir.dt.bfloat16
AF = mybir.ActivationFunctionType
ALU = mybir.AluOpType

D = 1024
NH = 16
FF = 2816
NJ = FF // 128
GW = 64
EPS = 1e-6
NEG = -30000.0

TR = 16
TT = TR * GW
HR = 5
EXT = (TR + 2 * HR) * GW
NCH = EXT // 128
QOFF = 256
NQ = 1152
NPAIR = 9
RES0 = 317
NRES = TT + 6
P_ROWS = 64
S_ROWS = 256
S_LOC = 32
NSLOT = 18
NB_EXT = 208
RES_BLK = [(0, 344), (344, 688), (688, 1030)]
NRM_BLK = [(0, 206), (206, 412), (412, 618), (618, 824), (824, 1030)]

W_Q, W_K, W_V, W_O, W_IN, W_OUT, W_UP0, W_UP1, NW8 = 0, 8, 16, 24, 32, 56, 64, 108, 152

V_ADAB = 0
V_N1 = 96
V_N2 = 112
V_FG = 128
V_BQK = 136
V_SCW = 152
V_FCW = 176
V_FCB = 308
NV = 352

COMPUTE_Q = ("pe", "act", "dve", "pool")
N_DMA_SEMS = 24


class Prog:
    def __init__(self, nc):
        self.nc = nc
        self.q = {k: [] for k in ("pe", "act", "dve", "pool", "sp")}
        self.res = {}
        self.dma_cnt = [0] * N_DMA_SEMS
        self.dma_rr = 0

    def _collect(self, reads, writes, me_q, is_dma):
        waits = set()
        for r in reads:
            st = self.res.get(r)
            if st is not None and st["w"] is not None:
                waits.add(st["w"])
        for wname in writes:
            st = self.res.get(wname)
            if st is None:
                continue
            w = st["w"]
            if w is not None and (is_dma or not (w[0] == "c" and w[1] == me_q)):
                waits.add(w)
            for qn, idx in st["rc"].items():
                if is_dma or qn != me_q:
                    waits.add(("c", qn, idx))
            for d in st["rd"]:
                waits.add(d)
        if not is_dma and me_q == "pe":
            waits = {w for w in waits if not (w[0] == "c" and w[1] == "pe")}
        return waits

    def _record(self, reads, writes, me):
        for r in reads:
            st = self.res.setdefault(r, {"w": None, "rc": {}, "rd": []})
            if me[0] == "c":
                st["rc"][me[1]] = me[2]
            else:
                st["rd"].append(me)
        for wname in writes:
            self.res[wname] = {"w": me, "rc": {}, "rd": []}

    def op(self, q, fn, reads=(), writes=()):
        waits = self._collect(reads, writes, q, False)
        idx = len(self.q[q])
        self.q[q].append({"fn": fn, "waits": waits, "inc": False, "dma": None})
        self._record(reads, writes, ("c", q, idx))
        return idx

    def dma(self, q, fn, reads=(), writes=()):
        waits = self._collect(reads, writes, q, True)
        k = self.dma_rr
        self.dma_rr = (self.dma_rr + 1) % N_DMA_SEMS
        if self.dma_cnt[k] > 0:
            waits.add(("d", k, 16 * self.dma_cnt[k]))
        self.dma_cnt[k] += 1
        me = ("d", k, 16 * self.dma_cnt[k])
        self.q[q].append({"fn": fn, "waits": waits, "inc": False, "dma": k})
        self._record(reads, writes, me)
        return me

    def emit(self):
        nc = self.nc
        for qn, ops in self.q.items():
            for o in ops:
                for w in o["waits"]:
                    if w[0] == "c":
                        self.q[w[1]][w[2]]["inc"] = True
        val = {}
        for qn, ops in self.q.items():
            c = 0
            for i, o in enumerate(ops):
                if o["dma"] is None and o["inc"]:
                    c += 1
                    val[(qn, i)] = c
        with ExitStack() as es:
            csem = {qn: es.enter_context(nc.semaphore("s_" + qn)) for qn in COMPUTE_Q}
            dsem = [es.enter_context(nc.semaphore("d%d" % i)) for i in range(N_DMA_SEMS)]
            block = es.enter_context(nc.Block())

            def run(qn, eng):
                seen_c = {}
                seen_d = {}
                for o in self.q[qn]:
                    for w in sorted(o["waits"]):
                        if w[0] == "c":
                            v = val[(w[1], w[2])]
                            if seen_c.get(w[1], 0) >= v:
                                continue
                            seen_c[w[1]] = v
                            eng.wait_ge(csem[w[1]], v)
                        else:
                            if seen_d.get(w[1], 0) >= w[2]:
                                continue
                            seen_d[w[1]] = w[2]
                            eng.wait_ge(dsem[w[1]], w[2])
                    ins = o["fn"](eng)
                    if o["dma"] is not None:
                        ins.then_inc(dsem[o["dma"]], 16)
                    elif o["inc"]:
                        ins.then_inc(csem[qn], 1)
                if qn == "sp":
                    for k in range(N_DMA_SEMS):
                        if self.dma_cnt[k] > 0:
                            eng.wait_ge(dsem[k], 16 * self.dma_cnt[k])

            @block.tensor
            def _(e):
                run("pe", e)

            @block.scalar
            def _(e):
                run("act", e)

            @block.vector
            def _(e):
                run("dve", e)

            @block.gpsimd
            def _(e):
                run("pool", e)

            @block.sync
            def _(e):
                run("sp", e)


def _win(r, rows):
    r = min(max(r, 0), rows - 1)
    rs = min(max(r - 4, 0), rows - 8)
    return rs, rs + 8


def _pair_valid(R0, rows, j):
    rA = R0 - 1 + 2 * j
    out = {}
    for c in range(NCH):
        k0 = R0 - 5 + 2 * c
        v = [[False, False], [False, False]]
        for qi, r in enumerate((rA, rA + 1)):
            lo, hi = _win(r, rows)
            for ki, k in enumerate((k0, k0 + 1)):
                v[qi][ki] = lo <= k < hi
        out[c] = v
    return out


def _pair_plan_static(R0, rows, j):
    val = _pair_valid(R0, rows, j)
    used = [c for c in range(NCH) if any(val[c][0]) or any(val[c][1])]
    c_lo, c_hi = min(used), max(used)
    n = c_hi - c_lo + 1
    s0 = 12 - 2 * (c_hi - j)
    assert 0 <= s0 and s0 + 2 * n <= NSLOT, (R0, rows, j, s0, n)
    rows_pv = [[], []]
    for i in range(n):
        c = c_hi - i
        for qi in range(2):
            t, b = val[c][qi]
            if t and b:
                rows_pv[qi].append((i, "both"))
            elif t:
                rows_pv[qi].append((i, "top"))
            elif b:
                rows_pv[qi].append((i, "bot"))
    return dict(c_hi=c_hi, n=n, s0=s0, pv=rows_pv, dyn=None)


def _sample_plans():
    plans = []
    masks = []
    for t in range(2):
        tp = []
        for j in range(NPAIR):
            per_core = [_pair_valid(S_LOC * c + TR * t, S_ROWS, j) for c in range(8)]
            same = all(per_core[c] == per_core[1] for c in range(8))
            if same:
                tp.append(_pair_plan_static(S_LOC * 1 + TR * t, S_ROWS, j))
                continue
            used = sorted({cc for pc in per_core for cc in range(NCH)
                           if any(pc[cc][0]) or any(pc[cc][1])})
            c_lo, c_hi = used[0], used[-1]
            n = c_hi - c_lo + 1
            assert n <= 7
            s0 = 12 - 2 * (c_hi - j)
            assert 0 <= s0 and s0 + 2 * n <= NSLOT
            m = np.zeros((8, 128, 896), np.float32)
            for c in range(8):
                for i in range(n):
                    cc = c_hi - i
                    for qi in range(2):
                        for ki in range(2):
                            if per_core[c][cc][qi][ki]:
                                m[c, ki * 64:(ki + 1) * 64, i * 128 + qi * 64:i * 128 + qi * 64 + 64] = 1.0
            tp.append(dict(c_hi=c_hi, n=n, s0=s0, pv=None, dyn=len(masks)))
            masks.append(m)
        plans.append(tp)
    return plans, masks


def build_program(tiles=None, debug=False):
    if tiles is None:
        tiles = [(0, t) for t in range(4)] + [(1, t) for t in range(2)]
    nc = bass.Bass("TRN2", target_bir_lowering=False)
    dr = {}
    splans, _masks = _sample_plans()
    n_dyn = len(_masks)

    def din(name, shape, dt=F32):
        dr[name] = nc.dram_tensor(name, list(shape), dt, kind="ExternalInput").ap()
        return dr[name]

    xp = din("xp", [128, 8, P_ROWS * GW + 2 * HR * GW])
    xs = din("xs", [128, 8, S_LOC * GW + 2 * HR * GW])
    cv = din("cv", [128, 8, 2])
    pcm = din("pcm", [128, 4])
    mt_in = din("mt", [128, n_dyn, 896])
    vecs_in = din("vecs", [128, NV])
    bvb_in = din("bvb", [128, D])
    ident_in = din("ident", [128, 128])
    btab_in = din("btab", [128, NH, NSLOT * 64])
    ada_in = din("ada", [96, 128, 1024])
    w8_in = din("w8", [NW8, 128, 1024])
    wd_in = din("wd", [16, 128, NJ * 128])
    yp = nc.dram_tensor("yp", [128, 8, P_ROWS * GW], F32, kind="ExternalOutput").ap()
    ys = nc.dram_tensor("ys", [128, 8, S_LOC * GW], F32, kind="ExternalOutput").ap()
    w8b = nc.dram_tensor("w8b", [NW8, 128, 1024], BF16, kind="Internal").ap()
    wdb = nc.dram_tensor("wdb", [16, 128, NJ * 128], BF16, kind="Internal").ap()
    tabb = nc.dram_tensor("tabb", [128, NH, NSLOT * 64], BF16, kind="Internal").ap()
    mtb = nc.dram_tensor("mtb", [128, n_dyn, 896], BF16, kind="Internal").ap()

    P = Prog(nc)

    with ExitStack() as es:
        def sb(name, shape, dt):
            return es.enter_context(nc.sbuf_tensor("sb_" + name, list(shape), dt))

        x_res = sb("x_res", [128, 8, NRES], F32)
        h_ext = sb("h_ext", [128, 8, EXT + 2], BF16)
        xblk = [sb("xblk%d" % i, [128, 8, 344], F32) for i in range(2)]
        rstd = sb("rstd", [128, EXT], F32)
        epsb = sb("epsb", [128, 1], F32)
        w8s = [sb("w8s%d" % i, [128, 8, 128], BF16) for i in range(5)]
        vecs = sb("vecs", [128, NV], F32)
        ident = sb("ident", [128, 128], BF16)
        ones = sb("ones", [128, 128], BF16)
        pcs = sb("pcs", [128, 4], F32)
        MOD = sb("MOD", [128, 2, 48, 2], F32)
        GM = sb("GM", [128, 2, 2, 2, 8], F32)
        G32 = sb("G32", [128, 3, 8], F32)
        FG32 = sb("FG32", [128, 8], F32)
        csil = sb("csil", [128, 8, 2], F32)
        fscr = sb("fscr", [128, 4], F32)
        rec = [sb("rec%d" % i, [128, 2], F32) for i in range(2)]
        bcolb = [sb("bcol%d" % i, [128, 1], F32) for i in range(4)]
        RX_ELEMS = 49000
        RX = sb("RX", [128, RX_ELEMS], BF16)
        ps = es.enter_context(nc.psum_tensor("ps", [128, 7 * 512], F32))
        psT = es.enter_context(nc.psum_tensor("psT", [128, 1024], BF16))

        off = [0]

        def carve(n_elems_bf16):
            o = off[0]
            off[0] += n_elems_bf16
            assert off[0] <= RX_ELEMS, off[0]
            return o

        def v_bf(o, n):
            return RX[:, o:o + n]

        def v_f32(o, n):
            return RX[:, o:o + 2 * n].bitcast(F32)

        offA_QT = [carve(NQ) for _ in range(2)]
        offA_KT = [carve(EXT) for _ in range(2)]
        offA_V = [carve(NCH * 130) for _ in range(2)]
        offA_OSB = carve(NPAIR * D)
        offA_TAB = [carve(2 * NSLOT * 64) for _ in range(2)]
        offA_E = [carve(896) for _ in range(2)]
        offA_P = [carve(896) for _ in range(2)]
        offA_OT = carve(8 * NQ)
        offA_MTS = carve(n_dyn * 896)
        offA_BVB = carve(2 * D)
        offA_STG = carve(2 * 2 * NSLOT * 64)
        endA = off[0]
        off[0] = 0
        offB_G = carve(NJ * NRES)
        offB_Z = carve(8 * NRES)
        offB_T = [[carve(2 * NRES + 8) for _ in range(3)] for _ in range(2)]
        offB_WD = [carve(NJ * 128) for _ in range(2)]
        endB = off[0]
        off[0] = 0

        QT = [v_bf(o, NQ) for o in offA_QT]
        KT = [v_bf(o, EXT) for o in offA_KT]
        VA = [v_bf(o, NCH * 130).rearrange("p (m h d) -> p m h d", h=2, d=65) for o in offA_V]
        OSB = v_bf(offA_OSB, NPAIR * D).rearrange("p (j f) -> p j f", f=D)
        TAB = [v_bf(o, 2 * NSLOT * 64).rearrange("p (h s) -> p h s", h=2) for o in offA_TAB]
        EB = [v_bf(o, 896) for o in offA_E]
        PB = [v_bf(o, 896) for o in offA_P]
        OT = v_bf(offA_OT, 8 * NQ).rearrange("p (k q) -> p k q", k=8)
        MTS = v_bf(offA_MTS, n_dyn * 896).rearrange("p (a b) -> p a b", a=n_dyn)
        BVB = v_f32(offA_BVB, D)
        STG = v_f32(offA_STG, 2 * NSLOT * 64).rearrange("p (h s) -> p h s", h=2)
        GB = v_bf(offB_G, NJ * NRES).rearrange("p (j n) -> p j n", j=NJ)
        ZB = v_bf(offB_Z, 8 * NRES).rearrange("p (k n) -> p k n", k=8)
        TMP = [[v_f32(o, NRES + 4) for o in oo] for oo in offB_T]
        WDS = [v_bf(o, NJ * 128).rearrange("p (j c) -> p j c", j=NJ) for o in offB_WD]

        RXL = "RX"
        dbg_outs = {}

        def dump(name, ap, reads):
            if not debug:
                return
            t = nc.dram_tensor("dbg_" + name, list(ap.shape), ap.dtype, kind="ExternalOutput").ap()
            dbg_outs[name] = t
            P.dma("sp", lambda e: e.dma_start(out=t, in_=ap), reads=list(reads) + [RXL], writes=[("dbg", name)])

        def fence():
            P.op("dve", lambda e: e.memset(fscr[:, 0:1], 0.0), reads=[], writes=[RXL])

        def vcol(c):
            return vecs[:, c:c + 1]

        bank_rr = [0]

        def bank(banks=(0, 1, 2, 3, 4, 5, 6)):
            b = banks[bank_rr[0] % len(banks)]
            bank_rr[0] += 1
            return b

        def pbank(b, n=512, p0=0, p1=128):
            return ps[p0:p1, b * 512:b * 512 + n]

        stream = []
        for (seg, t) in tiles:
            for c in range(8):
                stream += [("w8", W_Q + c), ("w8", W_K + c), ("w8", W_V + c)]
            stream += [("w8", W_O + o) for o in range(8)]
            for j in range(NJ):
                stream += [("w8", W_UP0 + j), ("w8", W_UP0 + NJ + j)]
            stream += [("wd", o) for o in range(8)]
            for k in range(8):
                stream += [("w8", W_IN + 8 + k), ("w8", W_IN + 16 + k), ("w8", W_IN + k)]
            stream += [("w8", W_OUT + o) for o in range(8)]
            for j in range(NJ):
                stream += [("w8", W_UP1 + j), ("w8", W_UP1 + NJ + j)]
            stream += [("wd", 8 + o) for o in range(8)]
        st = {"pos": 0, "emitted": 0, "n_w8": 0, "n_wd": 0, "slot": {}}
        LOOK = 4

        def _emit_load(i):
            kind, idx = stream[i]
            ensure_casts((kind, idx // 8 if kind == "w8" else idx // 4), 2)
            if kind == "w8":
                s = st["n_w8"] % 5
                st["n_w8"] += 1
                P.dma("sp", lambda e, s=s, idx=idx: e.dma_start(
                    out=w8s[s][:, :, :], in_=w8b[idx].rearrange("p (k c) -> p k c", k=8)),
                    reads=[("w8b", idx // 8)], writes=[("w8s", s)])
            else:
                s = st["n_wd"] % 2
                st["n_wd"] += 1
                P.dma("sp", lambda e, s=s, idx=idx: e.dma_start(
                    out=WDS[s], in_=wdb[idx].rearrange("p (j c) -> p j c", j=NJ)),
                    reads=[("wdb", idx // 4), RXL], writes=[("wds", s)])
            st["slot"][i] = s

        def wnext(kind, idx):
            i = st["pos"]
            assert stream[i] == (kind, idx), (i, stream[i], kind, idx)
            lim = min(len(stream), i + LOOK + 1)
            used = st.setdefault("used", {"w8": 0, "wd": 0})
            pool_sz = {"w8": 5, "wd": 2}
            while st["emitted"] < lim:
                k2 = stream[st["emitted"]][0]
                if k2 == "wd" and not st.get("wd_ok", False) and st["emitted"] > i:
                    break
                if st["n_" + k2] - used[k2] >= pool_sz[k2]:
                    break
                _emit_load(st["emitted"])
                st["emitted"] += 1
            if st["emitted"] <= i:
                _emit_load(i)
                st["emitted"] = i + 1
            st["pos"] += 1
            used[kind] += 1
            s = st["slot"][i]
            if kind == "w8":
                return w8s[s], ("w8s", s)
            return WDS[s], ("wds", s)

        P.dma("sp", lambda e: e.dma_start(out=vecs[:, :], in_=vecs_in[:, :]), writes=["vecs"])
        P.dma("sp", lambda e: e.dma_start(out=pcs[:, :], in_=pcm[:, :]), writes=["pcs"])
        P.dma("sp", lambda e: e.dma_start(out=csil[:, :, :], in_=cv[:, :, :]), writes=["csil"])
        P.op("dve", lambda e: e.memset(ones[:, :], 1.0), writes=["ones"])
        P.op("dve", lambda e: e.memset(epsb[:, :], float(D * EPS)), writes=["epsb"])
        csb = sb("csb", [128, 8, 2], BF16)
        ada_st = [sb("adast%d" % i, [128, 8, 128], BF16) for i in range(3)]
        P.op("act", lambda e: e.activation(out=csb[:, :, :], in_=csil[:, :, :], func=AF.Silu),
             reads=["csil"], writes=["csb"])
        ada_cnt = [0]

        def ada_chunk(l, oc, fixed_bank=None):
            ch = l * 48 + oc
            b = ada_cnt[0] % 3
            ada_cnt[0] += 1
            P.dma("pool", lambda e: e.dma_start(out=ada_st[b][:, :, :], in_=ada_in[ch].rearrange("p (k c) -> p k c", k=8)),
                  writes=[("adast", b)])
            pb = fixed_bank if fixed_bank is not None else bank()
            for kc in range(8):
                P.op("pe", lambda e, kc=kc: e.matmul(
                    pbank(pb, 2), lhsT=ada_st[b][:, kc, :], rhs=csb[:, kc, :],
                    start=(kc == 0), stop=(kc == 7)),
                    reads=[("adast", b), "csb"], writes=[("ps", pb)])
            P.op("dve", lambda e: e.tensor_scalar(
                out=MOD[:, l, oc, :], in0=pbank(pb, 2), scalar1=vcol(V_ADAB + l * 48 + oc), scalar2=None,
                op0=ALU.add), reads=[("ps", pb), "vecs"], writes=["MOD"])

        P.op("dve", lambda e: e.tensor_scalar(out=G32[:, 0:2, :], in0=vecs[:, V_N1:V_N1 + 16].rearrange("p (l k) -> p l k", l=2),
                                              scalar1=32.0, scalar2=None, op0=ALU.mult), reads=["vecs"], writes=["G32a"])
        P.op("dve", lambda e: e.tensor_scalar(out=FG32[:, :], in0=vecs[:, V_FG:V_FG + 8],
                                              scalar1=32.0, scalar2=None, op0=ALU.mult), reads=["vecs"], writes=["FG32"])
        G32b = sb("G32b", [128, 2, 8], F32)
        P.op("dve", lambda e: e.tensor_scalar(out=G32b[:, :, :], in0=vecs[:, V_N2:V_N2 + 16].rearrange("p (l k) -> p l k", l=2),
                                              scalar1=32.0, scalar2=None, op0=ALU.mult), reads=["vecs"], writes=["G32b"])

        def gm_op(l, which):
            for s in range(2):
                if which == 0:
                    P.op("dve", lambda e, s=s: e.scalar_tensor_tensor(
                        out=GM[:, l, 0, s, :], in0=MOD[:, l, 8:16, s], scalar=1.0, in1=G32[:, l, :],
                        op0=ALU.add, op1=ALU.mult), reads=["MOD", "G32a"], writes=[("GM", l, 0, s)])
                else:
                    P.op("dve", lambda e, s=s: e.scalar_tensor_tensor(
                        out=GM[:, l, 1, s, :], in0=MOD[:, l, 32:40, s], scalar=1.0, in1=G32b[:, l, :],
                        op0=ALU.add, op1=ALU.mult), reads=["MOD", "G32b"], writes=[("GM", l, 1, s)])

        ada_rest = []
        for l in range(2):
            for oc in range(48):
                if l == 0 and oc < 16:
                    continue
                ada_rest.append(lambda fb, l=l, oc=oc: ada_chunk(l, oc, fb))
                if oc == 15:
                    ada_rest.append(lambda fb, l=l: gm_op(l, 0))
                if oc == 39:
                    ada_rest.append(lambda fb, l=l: gm_op(l, 1))
        for oc in range(16):
            ada_chunk(0, oc)
        gm_op(0, 0)
        cast_order = ([("w8", g) for g in (0, 1, 2, 3)] + [("w8", g) for g in range(8, 14)] + [("wd", 0), ("wd", 1)]
                      + [("w8", g) for g in (4, 5, 6, 7)] + [("w8", g) for g in range(14, 19)] + [("wd", 2), ("wd", 3)])
        cast_pos = {k: i for i, k in enumerate(cast_order)}
        cast_state = {"n": 0}

        def ensure_casts(key, ahead=2):
            lim = min(len(cast_order), cast_pos[key] + 1 + ahead)
            while cast_state["n"] < lim:
                kind, g = cast_order[cast_state["n"]]
                cast_state["n"] += 1
                if kind == "w8":
                    P.dma("pool", lambda e, g=g: e.dma_start(
                        out=w8b[g * 8:(g + 1) * 8].rearrange("n p c -> (n p) c"),
                        in_=w8_in[g * 8:(g + 1) * 8].rearrange("n p c -> (n p) c")),
                        reads=[], writes=[("w8b", g)])
                else:
                    P.dma("pool", lambda e, g=g: e.dma_start(
                        out=wdb[g * 4:(g + 1) * 4].rearrange("n p c -> (n p) c"),
                        in_=wd_in[g * 4:(g + 1) * 4].rearrange("n p c -> (n p) c")),
                        reads=[], writes=[("wdb", g)])

        ensure_casts(("w8", 0), 2)
        P.dma("pool", lambda e: e.dma_start(out=ident[:, :], in_=ident_in[:, :]), writes=["ident"])
        P.dma("pool", lambda e: e.dma_start(out=mtb[:, :, :], in_=mt_in[:, :, :]), writes=["mtb"])
        tab_done = [False] * 8
        MODR = ["MOD"] + [("GM", l, w, s) for l in range(2) for w in range(2) for s in range(2)]
        dump("MOD", MOD[:, :, :, :], MODR)
        dump("GM", GM[:, :, :, :, :].rearrange("p a b c d -> p (a b c d)"), MODR)

        def modcol(l, oc, s):
            return MOD[:, l, oc, s:s + 1]

        def hgran(c0, c1):
            return [("h", i) for i in range(c0 // 128, (c1 + 127) // 128)]

        nb_rr = [0]
        EXT_BLK = [(0, 344), (344, 688), (688, 1032), (1032, 1376), (1376, 1664)]

        def _affine(i, n, gm_fn, sh_fn, dst_fn, dst_res):
            for k in range(8):
                gm = gm_fn(k)
                sh = sh_fn(k)
                if sh is None:
                    P.op("act", lambda e, k=k, gm=gm: e.activation(out=dst_fn(k), in_=xblk[i][:, k, 0:n], func=AF.Identity, scale=gm),
                         reads=[("xb", i)] + MODR + ["FG32"], writes=dst_res)
                else:
                    P.op("act", lambda e, k=k, gm=gm, sh=sh: e.activation(out=dst_fn(k), in_=xblk[i][:, k, 0:n], func=AF.Identity,
                                                                       scale=gm, bias=sh),
                         reads=[("xb", i)] + MODR, writes=dst_res)

        def _ss_and_rstd_begin(c0, c1, src, sres):
            n = c1 - c0
            P.op("act", lambda e: e.activation(out=h_ext[:, :, c0:c1], in_=src, func=AF.Square),
                 reads=sres, writes=hgran(c0, c1))
            pb = bank()
            for kc in range(8):
                P.op("pe", lambda e, kc=kc: e.matmul(pbank(pb, n), lhsT=ones[:, :], rhs=h_ext[:, kc, c0:c1],
                                                     start=(kc == 0), stop=(kc == 7)),
                     reads=hgran(c0, c1) + ["ones"], writes=[("ps", pb)])
            return pb

        def _rstd_finish(c0, c1, pb):
            n = c1 - c0
            rr = ("rstd", c0 // 344)
            P.op("act", lambda e: e.activation(out=rstd[:, c0:c1], in_=pbank(pb, n), func=AF.Ln, bias=epsb[:, 0:1], scale=1.0),
                 reads=[("ps", pb), "epsb"], writes=[rr])
            P.op("act", lambda e: e.activation(out=rstd[:, c0:c1], in_=rstd[:, c0:c1], func=AF.Exp, scale=-0.5),
                 reads=[rr], writes=[rr])

        def norm_ext(xin, pt0, s):
            pbs = []
            for (c0, c1) in EXT_BLK:
                n = c1 - c0
                i = nb_rr[0] % 2
                nb_rr[0] += 1
                P.dma("sp", lambda e, i=i, c0=c0, c1=c1, n=n: e.dma_start(out=xblk[i][:, :, 0:n], in_=xin[:, :, pt0 + c0:pt0 + c1]),
                      writes=[("xb", i)])
                pbs.append(_ss_and_rstd_begin(c0, c1, xblk[i][:, :, 0:n], [("xb", i)]))
            for b, (c0, c1) in enumerate(EXT_BLK):
                _rstd_finish(c0, c1, pbs[b])
            for b, (c0, c1) in enumerate(EXT_BLK):
                n = c1 - c0
                i = nb_rr[0] % 2
                nb_rr[0] += 1
                P.dma("sp", lambda e, i=i, c0=c0, c1=c1, n=n: e.dma_start(out=xblk[i][:, :, 0:n], in_=xin[:, :, pt0 + c0:pt0 + c1]),
                      writes=[("xb", i)])
                P.op("dve", lambda e, i=i, c0=c0, c1=c1, n=n: e.tensor_tensor(
                    out=xblk[i][:, :, 0:n], in0=xblk[i][:, :, 0:n],
                    in1=rstd[:, c0:c1].unsqueeze(1).to_broadcast([128, 8, n]), op=ALU.mult),
                    reads=[("xb", i), ("rstd", b)], writes=[("xb", i)])
                _affine(i, n, lambda k: GM[:, 0, 0, s, k:k + 1], lambda k: modcol(0, k, s),
                        lambda k, c0=c0, c1=c1: h_ext[:, k, c0:c1], hgran(c0, c1))

        SHC = NRES

        def norm_res(blocks, gm_fn, sh_ap, kind, yout=None, pt0=0, tag=None):
            while ss_pend:
                ss_pend.pop(0)()
            for b in (2, 0, 1):
                f0, f1 = RES_BLK[b]
                _rstd_finish(f0, f1, SSB[b])
            if sh_ap is not None:
                P.op("dve", lambda e: e.tensor_copy(out=h_ext[:, :, SHC:SHC + 1], in_=sh_ap),
                     reads=MODR, writes=[("h", SHC // 128)])
            for b in (2, 0, 1):
                c0, c1 = blocks[b]
                n = c1 - c0
                if kind == "h":
                    for k in range(8):
                        P.op("dve", lambda e, k=k, c0=c0, c1=c1: e.scalar_tensor_tensor(
                            out=h_ext[:, k, c0:c1], in0=x_res[:, k, c0:c1], scalar=gm_fn(k), in1=rstd[:, c0:c1],
                            op0=ALU.mult, op1=ALU.mult),
                            reads=xres_of(c0, c1) + [("rstd", c0 // 344)] + MODR, writes=hgran(c0, c1))
                else:
                    i = nb_rr[0] % 2
                    nb_rr[0] += 1
                    for k in range(8):
                        P.op("dve", lambda e, k=k, c0=c0, c1=c1, i=i, n=n: e.scalar_tensor_tensor(
                            out=xblk[i][:, k, 0:n], in0=x_res[:, k, c0:c1], scalar=gm_fn(k), in1=rstd[:, c0:c1],
                            op0=ALU.mult, op1=ALU.mult),
                            reads=xres_of(c0, c1) + [("rstd", c0 // 344), "FG32"], writes=[("xb", i)])
                    P.dma("sp", lambda e, i=i, n=n, c0=c0: e.dma_start(
                        out=yout[:, :, pt0 + c0 - 3:pt0 + c0 - 3 + n], in_=xblk[i][:, :, 0:n]),
                        reads=[("xb", i)], writes=[("y", tag, c0)])

        bc_rr = [0]

        def dense_fm_bias(wt, wres, evac):
            bi = bc_rr[0] % 4
            bc_rr[0] += 1
            bcol = bcolb[bi]
            order = [RES_BLK[2], RES_BLK[0], RES_BLK[1]]
            for oi, (c0, c1) in enumerate(order):
                pb = bank()
                ce = c1 + 1 if oi == 0 else c1
                n = ce - c0
                for kc in range(8):
                    P.op("pe", lambda e, kc=kc, pb=pb, c0=c0, ce=ce, n=n: e.matmul(
                        pbank(pb, n), lhsT=wt[:, kc, :], rhs=h_ext[:, kc, c0:ce], start=(kc == 0), stop=(kc == 7)),
                        reads=[wres] + hgran(c0, ce), writes=[("ps", pb)])
                if oi == 0:
                    P.op("act", lambda e, pb=pb, n=n: e.activation(out=bcol[:, :], in_=ps[:, pb * 512 + n - 1:pb * 512 + n], func=AF.Copy),
                         reads=[("ps", pb)], writes=[("bcol", bi)])
                evac(pb, c0, c1, bcol, ("bcol", bi))

        SSB = (4, 5, 6)
        ss_pend = []
        DBANKS = (0, 1, 2, 3)

        def resid_evac(gate_ap, oc, pb, c0, c1):
            n = c1 - c0
            P.op("dve", lambda e: e.scalar_tensor_tensor(
                out=x_res[:, oc, c0:c1], in0=pbank(pb, n), scalar=gate_ap, in1=x_res[:, oc, c0:c1],
                op0=ALU.mult, op1=ALU.add), reads=[("ps", pb)] + MODR + xres_of(c0, c1), writes=xres_of(c0, c1))
            b = [i for i, (a0, a1) in enumerate(RES_BLK) if a0 <= c0 < a1][0]
            f0, f1 = RES_BLK[b]
            nf = f1 - f0
            P.op("act", lambda e: e.activation(out=h_ext[:, oc, f0:f1], in_=x_res[:, oc, f0:f1], func=AF.Square),
                 reads=xres_of(f0, f1), writes=hgran(f0, f1) + [("sq", oc, b)])
            ss_pend.append(lambda: P.op("pe", lambda e: e.matmul(pbank(SSB[b], nf), lhsT=ones[:, :], rhs=h_ext[:, oc, f0:f1],
                                                                 start=(oc == 0), stop=(oc == 7)),
                                        reads=[("sq", oc, b), "ones"], writes=[("ps", SSB[b])]))

        def dense_fm(wt, wres, rhs_fn, rhs_res_fn, blocks, nk, evac, banks=(0, 1, 2, 3, 4, 5, 6)):
            for (c0, c1) in blocks:
                while len(ss_pend) > 8:
                    ss_pend.pop(0)()
                pb = bank(banks)
                n = c1 - c0
                for kc in range(nk):
                    P.op("pe", lambda e, kc=kc, pb=pb, c0=c0, c1=c1, n=n: e.matmul(
                        pbank(pb, n), lhsT=wt[:, kc, :], rhs=rhs_fn(kc, c0, c1), start=(kc == 0), stop=(kc == nk - 1)),
                        reads=[wres] + rhs_res_fn(c0, c1), writes=[("ps", pb)])
                evac(pb, c0, c1)

        def xres_of(c0, c1):
            return [("x", b) for b, (a0, a1) in enumerate(RES_BLK) if a0 < c1 and c0 < a1]

        def tile_prog(seg, t):
            s = seg
            xin = xp if seg == 0 else xs
            yout = yp if seg == 0 else ys
            rows = P_ROWS if seg == 0 else S_ROWS
            pt0 = TT * t
            first = (t == 0)
            last = (t == (3 if seg == 0 else 1))
            if seg == 0:
                plans = [_pair_plan_static(TR * t, P_ROWS, j) for j in range(NPAIR)]
            else:
                plans = splans[t]

            fence()
            st["wd_ok"] = False
            for b, (c0, c1) in enumerate(RES_BLK):
                P.dma("sp", lambda e, c0=c0, c1=c1: e.dma_start(out=x_res[:, :, c0:c1],
                                                              in_=xin[:, :, pt0 + RES0 + c0:pt0 + RES0 + c1]),
                      writes=[("x", b)])
            P.dma("sp", lambda e: e.dma_start(out=BVB, in_=bvb_in[:, :]), reads=[RXL], writes=["bvb"])
            if seg == 1:
                P.dma("sp", lambda e: e.dma_start(out=MTS, in_=mtb[:, :, :]), reads=["mtb", RXL], writes=["mts"])
            norm_ext(xin, pt0, s)
            dump("h_ext_%d_%d" % (seg, t), h_ext[:, :, :], hgran(0, EXT))
            for vb in range(2):
                P.op("dve", lambda e, vb=vb: e.memset(VA[vb][:, :, :, 64:65], 1.0), reads=[RXL], writes=[("va", vb)])
            QBLK = [(QOFF, QOFF + 384), (QOFF + 384, QOFF + 768), (QOFF + 768, QOFF + 1152)]
            KBLK = [(0, 416), (416, 832), (832, 1248), (1248, 1664)]

            def qkv_groups(c, fixed_bank):
                qb = c % 2
                hold = {}
                gs = []

                def pick():
                    return fixed_bank if fixed_bank is not None else bank()

                for bi, (c0, c1) in enumerate(QBLK):
                    def g(bi=bi, c0=c0, c1=c1):
                        if bi == 0:
                            if tab_done[c]:
                                P.dma("sp", lambda e: e.dma_start(out=TAB[qb], in_=tabb[:, 2 * c:2 * c + 2, :]),
                                      reads=[("tabb", c), RXL], writes=[("tab", qb)])
                            else:
                                tab_done[c] = True
                                P.dma("sp", lambda e: e.dma_start(out=STG, in_=btab_in[:, 2 * c:2 * c + 2, :]),
                                      reads=[RXL], writes=["stg"])
                                P.op("act", lambda e: e.activation(out=TAB[qb], in_=STG, func=AF.Exp),
                                     reads=["stg", RXL], writes=[("tab", qb)])
                                P.dma("sp", lambda e: e.dma_start(out=tabb[:, 2 * c:2 * c + 2, :], in_=TAB[qb]),
                                      reads=[("tab", qb), RXL], writes=[("tabb", c)])
                            hold["q"] = wnext("w8", W_Q + c)
                        wt, wres = hold["q"]
                        pb = pick()
                        n = c1 - c0
                        for kc in range(8):
                            P.op("pe", lambda e, kc=kc: e.matmul(
                                pbank(pb, n), lhsT=wt[:, kc, :], rhs=h_ext[:, kc, c0:c1], start=(kc == 0), stop=(kc == 7)),
                                reads=[wres] + hgran(c0, c1), writes=[("ps", pb)])
                        P.op("act", lambda e: e.activation(
                            out=QT[qb][:, c0 - QOFF:c1 - QOFF], in_=pbank(pb, n), func=AF.Identity, bias=vcol(V_BQK + c), scale=1.0),
                            reads=[("ps", pb), "vecs", RXL], writes=[("qt", qb)])
                    gs.append(g)
                for bi, (c0, c1) in enumerate(KBLK):
                    def g(bi=bi, c0=c0, c1=c1):
                        if bi == 0:
                            hold["k"] = wnext("w8", W_K + c)
                        wt, wres = hold["k"]
                        pb = pick()
                        n = c1 - c0
                        for kc in range(8):
                            P.op("pe", lambda e, kc=kc: e.matmul(
                                pbank(pb, n), lhsT=wt[:, kc, :], rhs=h_ext[:, kc, c0:c1], start=(kc == 0), stop=(kc == 7)),
                                reads=[wres] + hgran(c0, c1), writes=[("ps", pb)])
                        P.op("act", lambda e: e.activation(
                            out=KT[qb][:, c0:c1], in_=pbank(pb, n), func=AF.Identity, bias=vcol(V_BQK + 8 + c), scale=1.0),
                            reads=[("ps", pb), "vecs", RXL], writes=[("kt", qb)])
                    gs.append(g)
                for m0 in range(0, NCH, 4):
                    def g(m0=m0):
                        if m0 == 0:
                            hold["v"] = wnext("w8", W_V + c)
                        wt, wres = hold["v"]
                        mm = min(4, NCH - m0)
                        pb = pick()
                        for mi in range(mm):
                            m = m0 + mi
                            for kc in range(8):
                                P.op("pe", lambda e, kc=kc, m=m, mi=mi: e.matmul(
                                    ps[:, pb * 512 + mi * 128:pb * 512 + (mi + 1) * 128], lhsT=h_ext[:, kc, m * 128:(m + 1) * 128],
                                    rhs=wt[:, kc, :], start=(kc == 0), stop=(kc == 7)),
                                    reads=[wres, ("h", m)], writes=[("ps", pb)])
                        P.op("dve", lambda e: e.tensor_tensor(
                            out=VA[qb][:, m0:m0 + mm, :, 0:64],
                            in0=ps[:, pb * 512:pb * 512 + mm * 128].rearrange("p (m h d) -> p m h d", h=2, d=64),
                            in1=BVB[:, c * 128:(c + 1) * 128].rearrange("p (h d) -> p h d", h=2).unsqueeze(1).to_broadcast([128, mm, 2, 64]),
                            op=ALU.add), reads=[("ps", pb), "bvb", RXL], writes=[("va", qb)])
                    gs.append(g)
                return gs

            def emit_qk_sm(c, j, hh, ui):
                qb = c % 2
                pl = plans[j]
                n = pl["n"]
                sbk = (ui % 2) * 2
                eb = ui % 2
                hp0, hp1 = hh * 64, hh * 64 + 64
                for i in range(n):
                    cc = pl["c_hi"] - i
                    P.op("pe", lambda e, i=i, cc=cc: e.matmul(
                        ps[:, sbk * 512 + i * 128:sbk * 512 + (i + 1) * 128],
                        lhsT=KT[qb][hp0:hp1, cc * 128:(cc + 1) * 128],
                        rhs=QT[qb][hp0:hp1, j * 128:(j + 1) * 128], start=True, stop=True),
                        reads=[("kt", qb), ("qt", qb)], writes=[("ps", sbk), ("ps", sbk + 1)])
                P.op("act", lambda e: e.activation(
                    out=EB[eb][:, 0:n * 128], in_=ps[:, sbk * 512:sbk * 512 + n * 128], func=AF.Exp, scale=0.125),
                    reads=[("ps", sbk), ("ps", sbk + 1), RXL], writes=[("eb", eb)])
                s0 = pl["s0"]
                P.op("dve", lambda e: e.tensor_tensor(
                    out=PB[eb][:, 0:n * 128], in0=EB[eb][:, 0:n * 128],
                    in1=TAB[qb][:, hh, s0 * 64:s0 * 64 + n * 128], op=ALU.mult),
                    reads=[("eb", eb), ("tab", qb)], writes=[("pb", eb)])
                if pl["dyn"] is not None:
                    dy = pl["dyn"]
                    P.op("dve", lambda e: e.tensor_tensor(
                        out=PB[eb][:, 0:n * 128], in0=PB[eb][:, 0:n * 128], in1=MTS[:, dy, 0:n * 128], op=ALU.mult),
                        reads=[("pb", eb), "mts"], writes=[("pb", eb)])

            def emit_pv(c, j, hh, ui):
                qb = c % 2
                pl = plans[j]
                n = pl["n"]
                eb = ui % 2
                ob = 4 + (j % 2)
                if pl["dyn"] is not None:
                    for i in range(n):
                        cc = pl["c_hi"] - i
                        P.op("pe", lambda e, i=i, cc=cc: e.matmul(
                            ps[:, ob * 512 + hh * 65:ob * 512 + hh * 65 + 65],
                            lhsT=PB[eb][:, i * 128:(i + 1) * 128], rhs=VA[qb][:, cc, hh, :],
                            start=(i == 0), stop=(i == n - 1)),
                            reads=[("pb", eb), ("va", qb)], writes=[("ps", ob)])
                else:
                    for qi in range(2):
                        lst = pl["pv"][qi]
                        for li, (i, mode) in enumerate(lst):
                            cc = pl["c_hi"] - i
                            k0, k1 = {"both": (0, 128), "top": (0, 64), "bot": (64, 128)}[mode]
                            P.op("pe", lambda e, i=i, cc=cc, qi=qi, k0=k0, k1=k1, li=li, nl=len(lst): e.matmul(
                                ps[qi * 64:(qi + 1) * 64, ob * 512 + hh * 65:ob * 512 + hh * 65 + 65],
                                lhsT=PB[eb][k0:k1, i * 128 + qi * 64:i * 128 + qi * 64 + 64],
                                rhs=VA[qb][k0:k1, cc, hh, :], start=(li == 0), stop=(li == nl - 1)),
                                reads=[("pb", eb), ("va", qb)], writes=[("ps", ob)])
                if hh == 1:
                    rb = j % 2
                    ov = ps[:, ob * 512:ob * 512 + 130].rearrange("p (h d) -> p h d", d=65)
                    P.op("dve", lambda e: e.reciprocal(out=rec[rb][:, :], in_=ov[:, :, 64]),
                         reads=[("ps", ob)], writes=[("rec", rb)])
                    P.op("dve", lambda e: e.tensor_tensor(
                        out=OSB[:, j, c * 128:(c + 1) * 128].rearrange("p (h d) -> p h d", d=64),
                        in0=ov[:, :, 0:64], in1=rec[rb][:, :].unsqueeze(2).to_broadcast([128, 2, 64]), op=ALU.mult),
                        reads=[("ps", ob), ("rec", rb), RXL], writes=[("osb", j)])

            for g in qkv_groups(0, None):
                g()
            units = [(j, hh) for j in range(NPAIR) for hh in range(2)]
            for c in range(8):
                nxt = qkv_groups(c + 1, 6) if c < 7 else []
                for ui, (j, hh) in enumerate(units):
                    emit_qk_sm(c, j, hh, ui)
                    if ui >= 1:
                        emit_pv(c, units[ui - 1][0], units[ui - 1][1], ui - 1)
                    if nxt:
                        nxt.pop(0)()
                    if ada_rest:
                        ada_rest.pop(0)(6)
                emit_pv(c, units[-1][0], units[-1][1], len(units) - 1)
                while nxt:
                    nxt.pop(0)()
            while ada_rest:
                ada_rest.pop(0)(6)
            for j in range(NPAIR):
                for oc in range(8):
                    P.op("pe", lambda e, j=j, oc=oc: e.transpose(out=psT[:, oc * 128:(oc + 1) * 128],
                                                                 in_=OSB[:, j, oc * 128:(oc + 1) * 128], identity=ident[:, :]),
                         reads=[("osb", j), "ident"], writes=["psT"])
                P.op("act", lambda e, j=j: e.activation(out=OT[:, :, j * 128:(j + 1) * 128],
                                                        in_=psT[:, :].rearrange("p (k q) -> p k q", k=8), func=AF.Copy),
                     reads=["psT", RXL], writes=[("ot", j)])

            dump("OT_%d_%d" % (seg, t), OT, [("ot", jj) for jj in range(NPAIR)])
            dump("OSB_%d_%d" % (seg, t), OSB, [("osb", jj) for jj in range(NPAIR)])

            def otres(c0, c1):
                a, b = c0 + RES0 - QOFF, c1 + RES0 - QOFF
                return [("ot", jj) for jj in range(a // 128, (b + 127) // 128)]

            for oc in range(8):
                wt, wres = wnext("w8", W_O + oc)

                def evac(pb, c0, c1, oc=oc):
                    resid_evac(modcol(0, 16 + oc, s), oc, pb, c0, c1)
                dense_fm(wt, wres, lambda kc, c0, c1: OT[:, kc, c0 + RES0 - QOFF:c1 + RES0 - QOFF],
                         lambda c0, c1: otres(c0, c1) + [RXL], RES_BLK, 8, evac, banks=DBANKS)

            dump("x0m_%d_%d" % (seg, t), x_res[:, :, :], xres_of(0, NRES))
            fence()
            st["wd_ok"] = True
            ffn(0, s, seg, first, last)
            dump("x1_%d_%d" % (seg, t), x_res[:, :, :], xres_of(0, NRES))
            sc_mixer(s, seg, first, last)
            dump("x1m_%d_%d" % (seg, t), x_res[:, :, :], xres_of(0, NRES))
            ffn(1, s, seg, first, last)
            dump("x2_%d_%d" % (seg, t), x_res[:, :, :], xres_of(0, NRES))
            norm_res([(3, 344), (344, 688), (688, 1027)], lambda k: FG32[:, k:k + 1], None, "y",
                     yout=yout, pt0=pt0, tag=(seg, t))

        def pre_norm(l, which, s):
            base = 0 if which == 0 else 24
            norm_res(RES_BLK, lambda k: GM[:, l, which, s, k:k + 1], MOD[:, l, base:base + 8, s:s + 1], "h")

        def edge_fix(buf, res, seg, first, last):
            if seg == 0:
                if first:
                    P.op("dve", lambda e: e.memset(buf[:, 2:3], 0.0), reads=[], writes=[res])
                if last:
                    P.op("dve", lambda e: e.memset(buf[:, 1027:1028], 0.0), reads=[], writes=[res])
            else:
                if first:
                    P.op("dve", lambda e: e.tensor_scalar(out=buf[:, 2:3], in0=buf[:, 2:3], scalar1=pcs[:, 0:1], scalar2=None,
                                                          op0=ALU.mult), reads=[res, "pcs"], writes=[res])
                if last:
                    P.op("dve", lambda e: e.tensor_scalar(out=buf[:, 1027:1028], in0=buf[:, 1027:1028], scalar1=pcs[:, 1:2],
                                                          scalar2=None, op0=ALU.mult), reads=[res, "pcs"], writes=[res])

        def conv3(src, dst, w0, w1, w2, rs, rd):
            P.op("dve", lambda e: e.tensor_scalar(out=dst[:, 1:NRES - 1], in0=src[:, 1:NRES - 1], scalar1=w1, scalar2=None,
                                                  op0=ALU.mult), reads=[rs, "vecs"], writes=[rd])
            P.op("dve", lambda e: e.scalar_tensor_tensor(out=dst[:, 1:NRES - 1], in0=src[:, 0:NRES - 2], scalar=w0,
                                                         in1=dst[:, 1:NRES - 1], op0=ALU.mult, op1=ALU.add),
                 reads=[rs, rd, "vecs"], writes=[rd])
            P.op("dve", lambda e: e.scalar_tensor_tensor(out=dst[:, 1:NRES - 1], in0=src[:, 2:NRES], scalar=w2,
                                                         in1=dst[:, 1:NRES - 1], op0=ALU.mult, op1=ALU.add),
                 reads=[rs, rd, "vecs"], writes=[rd])

        def ffn(l, s, seg, first, last):
            pre_norm(l, 1, s)
            if l == 0:
                dump("h2", h_ext[:, :, 0:NRES], hgran(0, NRES))
            wup = W_UP0 if l == 0 else W_UP1
            for j in range(NJ):
                par = j % 2
                A, Vv, Tt = TMP[par]
                rA, rV, rT = ("tA", par), ("tV", par), ("tT", par)
                for (widx, dst, rdst) in ((wup + j, A, rA), (wup + NJ + j, Vv, rV)):
                    wt, wres = wnext("w8", widx)

                    def evac(pb, c0, c1, bcol, bres, dst=dst, rdst=rdst):
                        n = c1 - c0
                        P.op("act", lambda e: e.activation(out=dst[:, c0:c1], in_=pbank(pb, n), func=AF.Identity,
                                                           bias=bcol[:, 0:1], scale=1.0),
                             reads=[("ps", pb), bres, RXL], writes=[rdst])
                    dense_fm_bias(wt, wres, evac)
                edge_fix(A, rA, seg, first, last)
                if l == 0 and j == 0:
                    dump("ffn_A", A[:, 0:NRES], [rA])
                    dump("ffn_V", Vv[:, 0:NRES], [rV])
                cw = V_FCW + l * 66
                conv3(A, Tt, vcol(cw + j), vcol(cw + 22 + j), vcol(cw + 44 + j), rA, rT)
                P.op("act", lambda e, Tt=Tt, l=l, j=j: e.activation(out=Tt[:, 1:NRES - 1], in_=Tt[:, 1:NRES - 1], func=AF.Silu,
                                                                   bias=vcol(V_FCB + l * 22 + j), scale=1.0),
                     reads=[rT, "vecs", RXL], writes=[rT])
                P.op("dve", lambda e, Tt=Tt, Vv=Vv, j=j: e.tensor_tensor(out=GB[:, j, 1:NRES - 1], in0=Tt[:, 1:NRES - 1],
                                                                         in1=Vv[:, 1:NRES - 1], op=ALU.mult),
                     reads=[rT, rV, RXL], writes=[("g", j)])
                if l == 0 and j == 0:
                    dump("ffn_T", Tt[:, 0:NRES], [rT])
                    dump("ffn_G", GB[:, 0, :], [("g", 0)])
            DBLK = [(1, 344), (344, 688), (688, 1029)]
            for oc in range(8):
                wt, wres = wnext("wd", l * 8 + oc)

                def evac(pb, c0, c1, oc=oc):
                    resid_evac(modcol(l, 40 + oc, s), oc, pb, c0, c1)
                dense_fm(wt, wres, lambda kc, c0, c1: GB[:, kc, c0:c1], lambda c0, c1: [("g", jj) for jj in range(NJ)] + [RXL],
                         DBLK, NJ, evac, banks=DBANKS)

        def sc_mixer(s, seg, first, last):
            pre_norm(1, 0, s)
            for k in range(8):
                par = k % 2
                A, Vv, Tt = TMP[par]
                rA, rV, rT = ("tA", par), ("tV", par), ("tT", par)
                for (widx, dst, rdst) in ((W_IN + 8 + k, A, rA), (W_IN + 16 + k, Vv, rV)):
                    wt, wres = wnext("w8", widx)

                    def evac(pb, c0, c1, bcol, bres, dst=dst, rdst=rdst):
                        n = c1 - c0
                        P.op("act", lambda e: e.activation(out=dst[:, c0:c1], in_=pbank(pb, n), func=AF.Identity,
                                                           bias=bcol[:, 0:1], scale=1.0),
                             reads=[("ps", pb), bres, RXL], writes=[rdst])
                    dense_fm_bias(wt, wres, evac)
                P.op("dve", lambda e, A=A, Vv=Vv: e.tensor_tensor(out=A[:, 0:NRES], in0=A[:, 0:NRES], in1=Vv[:, 0:NRES], op=ALU.mult),
                     reads=[rA, rV], writes=[rA])
                edge_fix(A, rA, seg, first, last)
                conv3(A, Tt, vcol(V_SCW + k), vcol(V_SCW + 8 + k), vcol(V_SCW + 16 + k), rA, rT)
                def evac_b(pb, c0, c1, bcol, bres, Vv=Vv, rV=rV):
                    n = c1 - c0
                    P.op("act", lambda e: e.activation(out=Vv[:, c0:c1], in_=pbank(pb, n), func=AF.Identity,
                                                       bias=bcol[:, 0:1], scale=1.0),
                         reads=[("ps", pb), bres, RXL], writes=[rV])
                wtb, wrb = wnext("w8", W_IN + k)
                dense_fm_bias(wtb, wrb, evac_b)
                P.op("dve", lambda e, Tt=Tt, Vv=Vv, k=k: e.tensor_tensor(out=ZB[:, k, 1:NRES - 1], in0=Tt[:, 1:NRES - 1],
                                                                         in1=Vv[:, 1:NRES - 1], op=ALU.mult),
                     reads=[rT, rV, RXL], writes=[("z", k)])
            DBLK = [(1, 344), (344, 688), (688, 1029)]
            for oc in range(8):
                wt, wres = wnext("w8", W_OUT + oc)

                def evac(pb, c0, c1, oc=oc):
                    resid_evac(modcol(1, 16 + oc, s), oc, pb, c0, c1)
                dense_fm(wt, wres, lambda kc, c0, c1: ZB[:, kc, c0:c1], lambda c0, c1: [("z", kk) for kk in range(8)] + [RXL],
                         DBLK, 8, evac, banks=DBANKS)

        for (seg, t) in tiles:
            tile_prog(seg, t)
        P.emit()
    return nc, P


def _fm(x):
    T = x.shape[0]
    return np.ascontiguousarray(x.reshape(T, 8, 128).transpose(2, 1, 0))


def _wchunks(W):
    K, N = W.shape
    a = W.reshape(K // 128, 128, N // 128, 128).transpose(2, 1, 0, 3)
    return np.ascontiguousarray(a).reshape(N // 128, 128, (K // 128) * 128)


def _pvec(v):
    return np.ascontiguousarray(v.reshape(-1, 128).T)


def _bias_table(rpb):
    kc = np.arange(64)[:, None]
    qc = np.arange(64)[None, :]
    cs = np.clip(qc - 8, 0, 48)
    cvalid = (kc >= cs) & (kc < cs + 16)
    dc = np.clip(kc - qc + 15, 0, 30)
    out = np.full((128, NH, NSLOT, 64), NEG, np.float32)
    for s in range(NSLOT):
        for half, delta in ((0, 8 - s), (1, 9 - s)):
            if -7 <= delta <= 7:
                g = rpb[:, delta + 7][:, dc]
                g = np.where(cvalid[None], g, NEG).astype(np.float32)
                out[half * 64:(half + 1) * 64, :, s, :] = g.transpose(1, 0, 2)
    return out.reshape(128, NH, NSLOT * 64)


_CACHE = {}


def kernel(x_prompt, x_sample, c_prompt, c_sample, ada_w, ada_b, norm1_g, norm2_g,
           na_w_qkv, na_b_qkv, na_rpb, na_w_o, sc_w_in, sc_conv_w, sc_w_out,
           ffn_w_up, ffn_conv_w, ffn_conv_b, ffn_w_down, final_g, _tiles=None, _debug=False):
    f = lambda a: np.asarray(a, dtype=np.float32)
    x_prompt, x_sample, c_prompt, c_sample = f(x_prompt), f(x_sample), f(c_prompt), f(c_sample)
    key = (None if _tiles is None else tuple(_tiles), _debug)
    if key not in _CACHE:
        _CACHE[key] = build_program(_tiles, _debug)
    nc, _ = _CACHE[key]
    _, masks = _sample_plans()

    w8 = np.concatenate([
        _wchunks(f(na_w_qkv)[0]),
        _wchunks(f(na_w_o)[0]),
        _wchunks(f(sc_w_in)[0]),
        _wchunks(f(sc_w_out)[0]),
        _wchunks(f(ffn_w_up)[0]),
        _wchunks(f(ffn_w_up)[1]),
    ], 0)
    assert w8.shape[0] == NW8
    wd = np.stack([np.ascontiguousarray(
        f(ffn_w_down)[l].reshape(NJ, 128, 8, 128).transpose(2, 1, 0, 3)).reshape(8, 128, NJ * 128)
        for l in range(2)], 0).reshape(16, 128, NJ * 128)
    ada = np.concatenate([_wchunks(f(ada_w)[l]) for l in range(2)], 0)
    vecs = np.zeros((128, NV), np.float32)
    for l in range(2):
        vecs[:, V_ADAB + l * 48:V_ADAB + (l + 1) * 48] = _pvec(f(ada_b)[l])
        vecs[:, V_N1 + l * 8:V_N1 + (l + 1) * 8] = _pvec(f(norm1_g)[l])
        vecs[:, V_N2 + l * 8:V_N2 + (l + 1) * 8] = _pvec(f(norm2_g)[l])
        for tap in range(3):
            vecs[:, V_FCW + l * 66 + tap * 22:V_FCW + l * 66 + (tap + 1) * 22] = _pvec(f(ffn_conv_w)[l, tap])
        vecs[:, V_FCB + l * 22:V_FCB + (l + 1) * 22] = _pvec(f(ffn_conv_b)[l])
    vecs[:, V_FG:V_FG + 8] = _pvec(f(final_g))
    vecs[:, V_BQK:V_BQK + 16] = _pvec(f(na_b_qkv)[0, :2048])
    for tap in range(3):
        vecs[:, V_SCW + tap * 8:V_SCW + (tap + 1) * 8] = _pvec(f(sc_conv_w)[0, tap])
    bvb = np.ascontiguousarray(np.broadcast_to(f(na_b_qkv)[0, 2048:][None, :], (128, D)))
    ident = np.eye(128, dtype=np.float32)
    btab = _bias_table(f(na_rpb)[0])

    xs_pad = np.zeros((S_ROWS * GW + 2 * HR * GW, D), np.float32)
    xs_pad[HR * GW:HR * GW + S_ROWS * GW] = x_sample[0]
    in_maps = []
    for c in range(8):
        xpp = np.zeros((P_ROWS * GW + 2 * HR * GW, D), np.float32)
        xpp[HR * GW:HR * GW + P_ROWS * GW] = x_prompt[c]
        xsc = xs_pad[c * S_LOC * GW:c * S_LOC * GW + S_LOC * GW + 2 * HR * GW]
        cvv = np.stack([_pvec(c_prompt[c]), _pvec(c_sample[0])], -1)
        pcm = np.zeros((128, 4), np.float32)
        pcm[:, 0] = 0.0 if c == 0 else 1.0
        pcm[:, 1] = 0.0 if c == 7 else 1.0
        mt = np.stack([m[c] for m in masks], 1)
        in_maps.append({
            "xp": _fm(xpp), "xs": _fm(xsc), "cv": np.ascontiguousarray(cvv), "pcm": pcm, "mt": np.ascontiguousarray(mt),
            "vecs": vecs, "bvb": bvb, "ident": ident, "btab": btab, "ada": ada, "w8": w8, "wd": wd,
        })
    res = run_bass_kernel_spmd(nc, in_maps, core_ids=list(range(8)))
    y_p = np.empty((8, P_ROWS * GW, D), np.float32)
    y_s = np.empty((1, S_ROWS * GW, D), np.float32)
    for c in range(8):
        r = res.results[c]
        y_p[c] = np.asarray(r["yp"]).transpose(2, 1, 0).reshape(P_ROWS * GW, D)
        y_s[0, c * S_LOC * GW:(c + 1) * S_LOC * GW] = np.asarray(r["ys"]).transpose(2, 1, 0).reshape(S_LOC * GW, D)
    if _debug:
        return (y_p, y_s), res.results
    return (y_p, y_s)
```
